# Optimizing a Trainium2 kernel written in Bass

```python
import jax, jax.numpy as jnp
from jax import lax

D_MODEL = 1024
BATCH = 8
SEQ = 2048
DEPTH = 2
DEC_BATCH = 128
DEC_SEQ = 4
PAST_LEN = 16384
PAGE_SIZE = 128

N_MIXERS = 2
N_LAYERS_A = (DEPTH + N_MIXERS - 1) // N_MIXERS
N_LAYERS_B = DEPTH // N_MIXERS
D_RNN = D_MODEL
N_LRU_BLOCKS = 8
LRU_BLOCK = D_RNN // N_LRU_BLOCKS
CONV_W = 4
LRU_C = 8.0
D_POOL = D_MODEL
POOL_WINDOWS = (2, 4, 8, 16)
N_POOL_GROUPS = len(POOL_WINDOWS)
POOL_GW = D_POOL // N_POOL_GROUPS
POOL_PAST = max(POOL_WINDOWS) - 1
EPS = 1e-6

kernel_name = 'hybrid_rglru_pool_decode_step'


def rmsnorm(x, g):
    xf = x.astype(jnp.float32)
    y = xf * lax.rsqrt(jnp.mean(xf * xf, axis=-1, keepdims=True) + EPS)
    return (y * g.astype(jnp.float32)).astype(x.dtype)


def causal_conv(x, buf, w, b):
    ext = jnp.concatenate([buf.astype(x.dtype), x], axis=1)
    T = x.shape[1]
    y = ext[:, 0:T] * w[0]
    for k in range(1, CONV_W):
        y = y + ext[:, k:k + T] * w[k]
    return y + b, ext[:, -(CONV_W - 1):]


def rg_lru(xb, h0, pos, w_r, b_r, w_i, b_i, lam):
    B, T, _ = xb.shape
    xh = xb.reshape(B, T, N_LRU_BLOCKS, LRU_BLOCK)
    r = jax.nn.sigmoid((jnp.einsum('btni,nij->btnj', xh, w_r).reshape(B, T, D_RNN) + b_r).astype(jnp.float32))
    ig = jax.nn.sigmoid((jnp.einsum('btni,nij->btnj', xh, w_i).reshape(B, T, D_RNN) + b_i).astype(jnp.float32))
    log_a = -LRU_C * r * jax.nn.softplus(-lam.astype(jnp.float32))
    a = jnp.exp(log_a)
    mult = jnp.sqrt(-jnp.expm1(2.0 * log_a))
    mult = jnp.where(pos[None, :, None] == 0, 1.0, mult)
    bterm = mult * ig * xb.astype(jnp.float32)

    def step(h, ab):
        a_t, b_t = ab
        h = a_t * h + b_t
        return h, h

    hT, hs = lax.scan(step, h0.astype(jnp.float32), (jnp.swapaxes(a, 0, 1), jnp.swapaxes(bterm, 0, 1)))
    return jnp.swapaxes(hs, 0, 1), hT


def recurrent_layer(x, conv_buf, h0, pos, g, w_in, conv_w, conv_b, w_r, b_r, w_i, b_i, lam, w_out):
    u = rmsnorm(x, g)
    xb, gate = jnp.split(u @ w_in, 2, axis=-1)
    xc, new_buf = causal_conv(xb, conv_buf, conv_w, conv_b)
    hs, hT = rg_lru(xc, h0, pos, w_r, b_r, w_i, b_i, lam)
    y = (hs.astype(x.dtype) * jax.nn.silu(gate)) @ w_out
    return x + y, new_buf, hT.astype(x.dtype)


def pool_layer(x, pool_buf, pos, g, w_in, w_grp, b_grp, scale, w_out):
    B, T, _ = x.shape
    u = rmsnorm(x, g)
    xb, gate = jnp.split(u @ w_in, 2, axis=-1)
    ext = jnp.concatenate([pool_buf.astype(xb.dtype), xb], axis=1)
    cs = jnp.cumsum(ext.astype(jnp.float32), axis=1)
    cs = jnp.pad(cs, ((0, 0), (1, 0), (0, 0)))
    P = POOL_PAST
    means = []
    for gi, w in enumerate(POOL_WINDOWS):
        lo, hi = gi * POOL_GW, (gi + 1) * POOL_GW
        s = cs[:, P + 1:P + T + 1, lo:hi] - cs[:, P + 1 - w:P + T + 1 - w, lo:hi]
        cnt = jnp.minimum(pos + 1, w).astype(jnp.float32)[None, :, None]
        means.append(s / cnt)
    mean = jnp.stack(means, axis=2)
    pooled = (mean - xb.astype(jnp.float32).reshape(B, T, N_POOL_GROUPS, POOL_GW)).astype(x.dtype)
    z = jnp.einsum('btgi,gij->btgj', pooled, w_grp).reshape(B, T, D_POOL) + b_grp
    z = z * scale
    y = (z * jax.nn.silu(gate)) @ w_out
    return x + y, ext[:, -P:]


def _normal(k, shape, s):
    return jax.random.normal(k, shape, jnp.float32) * s


def setup_inputs(seed: int = 0) -> dict:
    key = jax.random.key(seed)
    ks = jax.random.split(key, 24)
    u = jax.random.uniform(ks[12], (N_LAYERS_A, D_RNN), jnp.float32, minval=0.9, maxval=0.999)
    a0 = u ** (1.0 / LRU_C)
    lam = jnp.log(a0) - jnp.log1p(-a0)
    return {
        'x_prompt': _normal(ks[0], (BATCH, SEQ, D_MODEL), 1.0),
        'x_sample': _normal(ks[1], (DEC_BATCH, DEC_SEQ, D_MODEL), 1.0),
        'state_conv': _normal(ks[2], (N_LAYERS_A, DEC_BATCH, CONV_W - 1, D_RNN), 1.0),
        'state_lru': _normal(ks[3], (N_LAYERS_A, DEC_BATCH, D_RNN), 0.3),
        'state_pool': _normal(ks[4], (N_LAYERS_B, DEC_BATCH, POOL_PAST, D_POOL), 1.0),
        'a_norm': 1.0 + _normal(ks[5], (N_LAYERS_A, D_MODEL), 0.05),
        'a_w_in': _normal(ks[6], (N_LAYERS_A, D_MODEL, 2 * D_RNN), D_MODEL ** -0.5),
        'a_conv_w': _normal(ks[7], (N_LAYERS_A, CONV_W, D_RNN), CONV_W ** -0.5),
        'a_conv_b': _normal(ks[8], (N_LAYERS_A, D_RNN), 0.01),
        'a_w_r': _normal(ks[9], (N_LAYERS_A, N_LRU_BLOCKS, LRU_BLOCK, LRU_BLOCK), LRU_BLOCK ** -0.5),
        'a_b_r': _normal(ks[10], (N_LAYERS_A, D_RNN), 0.01),
        'a_w_i': _normal(ks[11], (N_LAYERS_A, N_LRU_BLOCKS, LRU_BLOCK, LRU_BLOCK), LRU_BLOCK ** -0.5),
        'a_b_i': _normal(ks[13], (N_LAYERS_A, D_RNN), 0.01),
        'a_lam': lam,
        'a_w_out': _normal(ks[14], (N_LAYERS_A, D_RNN, D_MODEL), D_RNN ** -0.5),
        'b_norm': 1.0 + _normal(ks[15], (N_LAYERS_B, D_MODEL), 0.05),
        'b_w_in': _normal(ks[16], (N_LAYERS_B, D_MODEL, 2 * D_POOL), D_MODEL ** -0.5),
        'b_w_grp': _normal(ks[17], (N_LAYERS_B, N_POOL_GROUPS, POOL_GW, POOL_GW), POOL_GW ** -0.5),
        'b_b_grp': _normal(ks[18], (N_LAYERS_B, D_POOL), 0.01),
        'b_scale': 1.0 + _normal(ks[19], (N_LAYERS_B, D_POOL), 0.1),
        'b_w_out': _normal(ks[20], (N_LAYERS_B, D_POOL, D_MODEL), D_POOL ** -0.5),
        'final_norm': 1.0 + _normal(ks[21], (D_MODEL,), 0.05),
    }


def reference(x_prompt, x_sample, state_conv, state_lru, state_pool,
              a_norm, a_w_in, a_conv_w, a_conv_b, a_w_r, a_b_r, a_w_i, a_b_i, a_lam, a_w_out,
              b_norm, b_w_in, b_w_grp, b_b_grp, b_scale, b_w_out, final_norm):
    Bp, Tp, _ = x_prompt.shape
    Ts = x_sample.shape[1]
    pos_p = jnp.arange(Tp, dtype=jnp.int32)
    pos_s = PAST_LEN + jnp.arange(Ts, dtype=jnp.int32)
    hp, hs = x_prompt, x_sample
    conv_p, lru_p, pool_p = [], [], []
    conv_s, lru_s, pool_s = [], [], []
    for layer in range(DEPTH):
        j = layer // N_MIXERS
        if layer % N_MIXERS == 0:
            wa = (a_norm[j], a_w_in[j], a_conv_w[j], a_conv_b[j], a_w_r[j], a_b_r[j],
                  a_w_i[j], a_b_i[j], a_lam[j], a_w_out[j])
            zc = jnp.zeros((Bp, CONV_W - 1, D_RNN), x_prompt.dtype)
            zh = jnp.zeros((Bp, D_RNN), jnp.float32)
            hp, c, h = recurrent_layer(hp, zc, zh, pos_p, *wa)
            conv_p.append(c)
            lru_p.append(h)
            hs, c, h = recurrent_layer(hs, state_conv[j], state_lru[j], pos_s, *wa)
            conv_s.append(c)
            lru_s.append(h)
        else:
            wb = (b_norm[j], b_w_in[j], b_w_grp[j], b_b_grp[j], b_scale[j], b_w_out[j])
            zp = jnp.zeros((Bp, POOL_PAST, D_POOL), x_prompt.dtype)
            hp, pb = pool_layer(hp, zp, pos_p, *wb)
            pool_p.append(pb)
            hs, pb = pool_layer(hs, state_pool[j], pos_s, *wb)
            pool_s.append(pb)
    y_prompt = rmsnorm(hp, final_norm)
    y_sample = rmsnorm(hs, final_norm)
    return (y_prompt, y_sample,
            jnp.stack(conv_p), jnp.stack(lru_p), jnp.stack(pool_p),
            jnp.stack(conv_s), jnp.stack(lru_s), jnp.stack(pool_s))
```

```python
import numpy as np
from contextlib import ExitStack
import concourse.bass as bass
import concourse.mybir as mybir
from concourse.bass_utils import run_bass_kernel_spmd

F32 = mybir.dt.float32
BF16 = mybir.dt.bfloat16
I32 = mybir.dt.int32
AF = mybir.ActivationFunctionType
ALU = mybir.AluOpType

NCORES = 8
D = 1024
NCH = 8
SEQ = 2048
T = 512
NSEQ_S = 16
STEPS = 4
TS = NSEQ_S * STEPS
EPS = 1e-6
POOL_W = (2, 4, 8, 16)
LN_HALF = float(np.log(0.5))

V_ANORM, V_CW0, V_CB, V_BR, V_BI, V_LAM, V_BNORM, V_BGRP, V_BSCALE = 0, 1, 5, 6, 7, 8, 9, 10, 11


class Ev:
    __slots__ = ("sem", "val")

    def __init__(self, sem, val):
        self.sem = sem
        self.val = val


class Buf:
    __slots__ = ("name", "w", "r")

    def __init__(self, name):
        self.name = name
        self.w = None
        self.r = {}

    def add_read(self, ev):
        if ev is not None:
            self.r[ev.sem] = max(self.r.get(ev.sem, 0), ev.val)

    def readers(self):
        return [Ev(s, v) for s, v in self.r.items()]


class Sched:
    ENGS = ("pe", "act", "dve", "pool", "sp")

    def __init__(self, nc, es):
        self.nc = nc
        self.es = es
        self.ops = {e: [] for e in self.ENGS}
        self.sems = {}
        self.semcnt = {}
        self.waited = {e: {} for e in self.ENGS}
        for e in self.ENGS:
            self.newsem("E_" + e)

    def newsem(self, key):
        self.sems[key] = self.es.enter_context(self.nc.semaphore(key))
        self.semcnt[key] = 0
        return key

    def _waits(self, eng, deps):
        need = {}
        for d in deps:
            if d is None:
                continue
            need[d.sem] = max(need.get(d.sem, 0), d.val)
        w = []
        for s, v in need.items():
            if self.waited[eng].get(s, 0) < v:
                self.waited[eng][s] = v
                w.append((s, v))
        return w

    def _hazards(self, reads, writes, deps):
        al = list(deps)
        for b in reads:
            al.append(b.w)
        for b in writes:
            al.append(b.w)
            al.extend(b.readers())
        return al

    def _commit(self, ev, reads, writes):
        for b in reads:
            b.add_read(ev)
        for b in writes:
            b.w = ev
            b.r = {}

    def run(self, eng, fn, reads=(), writes=(), deps=()):
        w = self._waits(eng, self._hazards(reads, writes, deps))
        key = "E_" + eng
        self.semcnt[key] += 1
        ev = Ev(key, self.semcnt[key])
        self.ops[eng].append((w, fn, key, 1))
        self._commit(ev, reads, writes)
        return ev

    def dma(self, fn, semkey, reads=(), writes=(), deps=(), eng="sp"):
        w = self._waits(eng, self._hazards(reads, writes, deps))
        self.semcnt[semkey] += 16
        ev = Ev(semkey, self.semcnt[semkey])
        self.ops[eng].append((w, fn, semkey, 16))
        self._commit(ev, reads, writes)
        return ev

    def wait_only(self, eng, deps):
        w = self._waits(eng, deps)
        if w:
            self.ops[eng].append((w, None, None, 0))

    def emit(self):
        nc = self.nc
        sems = self.sems

        def run(engname):
            def body(eng):
                for (w, fn, inc, amt) in self.ops[engname]:
                    for (s, v) in w:
                        eng.wait_ge(sems[s], v)
                    if fn is None:
                        continue
                    inst = fn(eng)
                    inst.then_inc(sems[inc], amt)
            return body

        with nc.Block() as block:
            block.tensor(run("pe"))
            block.scalar(run("act"))
            block.vector(run("dve"))
            block.gpsimd(run("pool"))
            block.sync(run("sp"))


def build_program():
    nc = bass.Bass("TRN2", target_bir_lowering=False)

    def din(name, shape):
        return nc.dram_tensor(name, shape, F32, kind="ExternalInput").ap()

    def dout(name, shape):
        return nc.dram_tensor(name, shape, F32, kind="ExternalOutput").ap()

    x_p = din("x_p", [SEQ, D])
    x_s = din("x_s", [NSEQ_S, STEPS, D])
    st_conv = din("st_conv", [NSEQ_S, 3, D])
    st_lru = din("st_lru", [NSEQ_S, D])
    st_pool = din("st_pool", [NSEQ_S, 15, D])
    vecs = din("vecs", [16, D])
    fnorm = din("fnorm", [1, D])
    a_w_in = din("a_w_in", [D, 2 * D])
    a_w_r = din("a_w_r", [8, 128, 128])
    a_w_i = din("a_w_i", [8, 128, 128])
    a_w_out = din("a_w_out", [D, D])
    b_w_in = din("b_w_in", [D, 2 * D])
    b_w_grp = din("b_w_grp", [4, 256, 256])
    b_w_out = din("b_w_out", [D, D])

    y_p = dout("y_p", [SEQ, D])
    y_s = dout("y_s", [NSEQ_S, STEPS, D])
    o_conv_p = dout("o_conv_p", [3, D])
    o_lru_p = dout("o_lru_p", [1, D])
    o_pool_p = dout("o_pool_p", [15, D])
    o_conv_s = dout("o_conv_s", [NSEQ_S, 3, D])
    o_lru_s = dout("o_lru_s", [NSEQ_S, D])
    o_pool_s = dout("o_pool_s", [NSEQ_S, 15, D])

    with ExitStack() as es:
        S = Sched(nc, es)

        def sb(name, shape, dt):
            return es.enter_context(nc.sbuf_tensor(name, shape, dt))

        W_in = [sb("W_in0", [128, NCH, 2 * D], BF16), sb("W_in1", [128, NCH, 2 * D], BF16)]
        W_out = [sb("W_out0", [128, NCH, D], BF16), sb("W_out1", [128, NCH, D], BF16)]
        W_r = sb("W_r", [128, NCH, 128], BF16)
        W_i = sb("W_i", [128, NCH, 128], BF16)
        W_grp = sb("W_grp", [128, 8, 256], BF16)
        XBt = [sb("XB0", [128, 4, D], F32), sb("XB1", [128, 4, D], F32)]
        XSreg = sb("XSreg", [128, D], F32)
        UT = sb("UT", [128, NCH, T], BF16)
        HGt = [sb("HG0", [128, NCH, T], BF16), sb("HG1", [128, NCH, T], BF16)]
        HG = HGt[0]
        EXTW = 15 + T + 1
        EXTt = [sb("EXT%d" % i, [128, EXTW], F32) for i in range(2)]
        TMP = sb("TMP", [128, 4096], F32)
        TRt = [sb("TR%d" % i, [128, T], F32) for i in range(2)]
        XCBt = [sb("XCB%d" % i, [128, T], BF16) for i in range(2)]
        A2Mt = [sb("A2M%d" % i, [128, T], F32) for i in range(2)]
        Ht = [sb("H%d" % i, [128, T], F32) for i in range(2)]
        FN = sb("FN", [128, D], F32)
        VT = sb("VT", [128, NCH, 16], F32)
        DER = sb("DER", [128, 4, NCH], F32)
        DER2 = sb("DER2", [128, 2, NCH], F32)
        IDENT = sb("IDENT", [128, 128], F32)
        IDENTB = sb("IDENTB", [128, 128], BF16)
        IOTI = sb("IOTI", [128, 16], I32)
        RC = sb("RC", [128, 4, 16], F32)
        SSt = sb("SS", [128, 16], F32)
        TMP16 = sb("TMP16", [128, 16], F32)
        NEGH = sb("NEGH", [128, 4], F32)
        JUNK = sb("JUNK", [128, D], mybir.dt.uint8)
        H0p = sb("H0p", [128, NCH, 3], F32)
        H1p = sb("H1p", [128, NCH, 15], F32)
        HSTp = sb("HSTp", [128, NCH, 1], F32)
        H0s = sb("H0s", [128, NCH, 3 * NSEQ_S], F32)
        H1s = HGt[0][:].rearrange("p c t -> p (c t)").bitcast(F32)[:, 0:NCH * 15 * NSEQ_S].rearrange(
            "p (c r) -> p c r", r=15 * NSEQ_S)
        HSTs = sb("HSTs", [128, NCH, NSEQ_S], F32)

        PS = [es.enter_context(nc.psum_tensor("PS%d" % i, [128, 512], F32)) for i in range(8)]

        bW_in = [[Buf("win%d_%d" % (l, j)) for j in range(8)] for l in range(2)]
        bW_out = [[Buf("wout%d_%d" % (l, q)) for q in range(2)] for l in range(2)]
        bW_r, bW_i, bW_grp = Buf("wr"), Buf("wi"), Buf("wgrp")
        bXB = [[Buf("xb%d_%d" % (b, i)) for i in range(4)] for b in range(2)]
        bXS = [Buf("xs0"), Buf("xs1")]
        bUTs = [Buf("ut%d" % i) for i in range(4)]
        bHG2 = [[Buf("hg%d_%d" % (p_, c)) for c in range(NCH)] for p_ in range(2)]
        bHGc = bHG2[0]
        bJUNK = Buf("junk")
        bNEGH = Buf("negh")
        N2_ENG = "pool"
        N4_ENG = ("act", "dve")
        bEXT = [Buf("ext0"), Buf("ext1")]
        bSG = [Buf("sg%d" % i) for i in range(4)]
        bV = [Buf("v%d" % i) for i in range(2)]
        bXC = [Buf("xc%d" % i) for i in range(2)]
        bTR = [Buf("tr%d" % i) for i in range(2)]
        bXCB = [Buf("xcb%d" % i) for i in range(2)]
        bA2M = [Buf("a2m%d" % i) for i in range(2)]
        bH = [Buf("h%d" % i) for i in range(2)]
        bPS = [Buf("ps%d" % i) for i in range(8)]
        bFN, bVT, bDER, bID, bIDB, bRC, bT16 = (Buf("fn"), Buf("vt"), Buf("der"), Buf("id"), Buf("idb"),
                                                 Buf("rc"), Buf("t16"))
        bSSc = [Buf("ss%d" % i) for i in range(16)]
        bH0 = {True: [Buf("h0p%d" % c) for c in range(NCH)], False: [Buf("h0s%d" % c) for c in range(NCH)]}
        bH1 = {True: [Buf("h1p%d" % c) for c in range(NCH)], False: [Buf("h1s%d" % c) for c in range(NCH)]}
        bHST = {True: [Buf("hstp%d" % c) for c in range(NCH)], False: [Buf("hsts%d" % c) for c in range(NCH)]}

        XS = [XSreg[:, 0:512].bitcast(BF16), XSreg[:, 512:1024].bitcast(BF16)]
        SG = [TMP[:, i * 512:(i + 1) * 512] for i in range(4)]
        V = [TMP[:, 2048 + i * 512:2048 + (i + 1) * 512] for i in range(2)]
        XC = [TMP[:, 3072 + i * 512:3072 + (i + 1) * 512] for i in range(2)]
        SR = [TMP[:, i * 1024:(i + 1) * 1024] for i in range(4)]
        bSR = [[bSG[0], bSG[1]], [bSG[2], bSG[3]], [bV[0], bV[1]], [bXC[0], bXC[1]]]
        PSB = [p[:].bitcast(BF16) for p in PS]

        for k in ("ld_vecs", "ld_fn", "ld_x0", "ld_x1", "st_x0", "st_x1", "wst0", "wst1", "ld_sr", "st_m0", "st_m1", "st_m2", "st_m3"):
            S.newsem(k)

        for s_ in range(4):
            k_ = S.newsem("ld_x0s%d" % s_)
            S.dma(lambda e, s_=s_: e.dma_start(out=XBt[0][:, s_, :], in_=x_p[s_ * 128:(s_ + 1) * 128, :]), k_,
                  writes=[bXB[0][s_]])
        VROWS = XSreg[0:16, :]
        S.dma(lambda e: e.dma_start(out=VROWS, in_=vecs), "ld_vecs", writes=[bXS[0], bXS[1]])
        S.dma(lambda e: e.dma_start(out=FN[:], in_=fnorm.partition_broadcast(128)), "ld_fn", writes=[bFN])

        S.run("pool", lambda e: e.memset(NEGH[:], -0.5), writes=[bNEGH])
        S.run("pool", lambda e: e.memset(IDENT[:], 0.0), writes=[bID])
        S.run("pool", lambda e: e.affine_select(out=IDENT[:], in_=IDENT[:], pattern=[[-1, 128]],
                                                compare_op=ALU.not_equal, fill=1.0, base=0,
                                                channel_multiplier=1), reads=[bID], writes=[bID])
        S.run("dve", lambda e: e.tensor_copy(out=IDENTB[:], in_=IDENT[:]), reads=[bID], writes=[bIDB])
        for t_, b_ in ((H0p, bH0[True]), (H1p, bH1[True]), (HSTp, bHST[True])):
            S.run("pool", lambda e, t_=t_: e.memset(t_[:], 0.0), writes=b_)

        def tr_vecs(e):
            last = None
            for c in range(NCH):
                last = e.transpose(out=PS[6][:, c * 16:(c + 1) * 16], in_=VROWS[:, c * 128:(c + 1) * 128],
                                   identity=IDENT[0:16, 0:16])
            return last
        S.run("pe", tr_vecs, reads=[bXS[0], bXS[1], bID], writes=[bPS[6]])
        S.run("dve", lambda e: e.tensor_copy(out=VT[:].rearrange("p c r -> p (c r)"), in_=PS[6][:, 0:128]),
              reads=[bPS[6]], writes=[bVT])

        def vcol(c, r):
            return VT[:, c, r:r + 1]

        lamv = VT[:, :, V_LAM]
        K1, HBR, HBI, SCR = DER[:, 0, :], DER[:, 1, :], DER[:, 2, :], DER[:, 3, :]
        BGS = DER2[:, 1, :]
        K2 = DER2[:, 0, :]

        def late_prologue():
            S.run("act", lambda e: e.activation(out=SCR, in_=lamv, func=AF.Abs), reads=[bVT], writes=[bDER])
            S.run("act", lambda e: e.activation(out=SCR, in_=SCR, func=AF.Exp, scale=-1.0), reads=[bDER], writes=[bDER])
            S.run("act", lambda e: e.activation(out=SCR, in_=SCR, func=AF.Ln, bias=1.0), reads=[bDER], writes=[bDER])
            S.run("dve", lambda e: e.tensor_scalar(out=K1, in0=lamv, scalar1=-1.0, scalar2=0.0, op0=ALU.mult,
                                                   op1=ALU.max), reads=[bVT, bDER], writes=[bDER])
            S.run("dve", lambda e: e.tensor_tensor(out=K1, in0=K1, in1=SCR, op=ALU.add), reads=[bDER], writes=[bDER])
            S.run("dve", lambda e: e.tensor_scalar(out=K1, in0=K1, scalar1=-4.0, scalar2=None, op0=ALU.mult),
                  reads=[bDER], writes=[bDER])
            S.run("dve", lambda e: e.tensor_tensor(out=BGS, in0=VT[:, :, V_BGRP], in1=VT[:, :, V_BSCALE], op=ALU.mult),
                  reads=[bVT], writes=[bDER])
            S.run("dve", lambda e: e.tensor_scalar(out=K2, in0=K1, scalar1=2.0, scalar2=None, op0=ALU.mult),
                  reads=[bDER], writes=[bDER])
            S.run("dve", lambda e: e.tensor_scalar(out=HBR, in0=VT[:, :, V_BR], scalar1=0.5, scalar2=None,
                                                   op0=ALU.mult), reads=[bVT, bDER], writes=[bDER])
            S.run("dve", lambda e: e.tensor_scalar(out=HBI, in0=VT[:, :, V_BI], scalar1=0.5, scalar2=None,
                                                   op0=ALU.mult), reads=[bVT, bDER], writes=[bDER])
            S.run("pool", lambda e: e.iota(IOTI[:], pattern=[[1, 16]], base=1, channel_multiplier=0), writes=[bRC])

        rc_done = set()

        def rc_group(g):
            if g in rc_done:
                return
            rc_done.add(g)
            Wd = POOL_W[g]
            S.run("dve", lambda e: e.tensor_copy(out=RC[:, g, :], in_=IOTI[:]), reads=[bRC], writes=[bRC])
            S.run("dve", lambda e: e.tensor_scalar(out=RC[:, g, :], in0=RC[:, g, :], scalar1=float(Wd),
                                                   scalar2=None, op0=ALU.min), reads=[bRC], writes=[bRC])
            S.run("dve", lambda e: e.reciprocal(out=RC[:, g, :], in_=RC[:, g, :]), reads=[bRC], writes=[bRC])


        def rows_to_fm(src, bsrc, nrows, dst_fn, bdst, psi, multi=False, eng="dve"):
            for half in range(2):
                def trf(e, half=half):
                    last = None
                    for cc in range(4):
                        c = half * 4 + cc
                        last = e.transpose(out=PS[psi][:, cc * 128:cc * 128 + nrows],
                                           in_=src[0:nrows, c * 128:(c + 1) * 128],
                                           identity=IDENT[0:nrows, 0:nrows])
                    return last
                S.run("pe", trf, reads=list(bsrc) + [bID], writes=[bPS[psi]])
                for cc in range(4):
                    c = half * 4 + cc
                    if eng == "dve":
                        S.run("dve", lambda e, c=c, cc=cc: e.tensor_copy(out=dst_fn(c), in_=PS[psi][:, cc * 128:cc * 128 + nrows]),
                              reads=[bPS[psi]], writes=(bdst[c] if multi else [bdst[c]]))
                    else:
                        S.run("act", lambda e, c=c, cc=cc: e.activation(out=dst_fn(c), in_=PS[psi][:, cc * 128:cc * 128 + nrows],
                                                                        func=AF.Copy),
                              reads=[bPS[psi]], writes=(bdst[c] if multi else [bdst[c]]))

        wq = []

        def wdma(src, dst, bdst, defer):
            k = S.newsem("w%d" % len(S.sems))
            bl = list(bdst) if isinstance(bdst, (list, tuple)) else [bdst]

            def go():
                S.dma(lambda e: e.dma_start(out=dst, in_=src), k, writes=bl, eng="pool")
            if defer:
                wq.append(go)
            else:
                go()

        def w_in_view(w, half, blk):
            c0 = half * D + blk * 512
            return w.rearrange("(kc p) n -> p kc n", p=128)[:, :, c0:c0 + 512], c0

        def load_layer_weights(l, w_in_d, w_out_d, defer):
            for blk in range(2):
                if l == 0 and blk == 0:
                    wv_ = w_in_d.rearrange("(kc p) n -> p kc n", p=128)
                    for pr_ in range(2):
                        for half in range(2):
                            c0 = half * D + pr_ * 256
                            wdma(wv_[:, :, c0:c0 + 256], W_in[l][:, :, c0:c0 + 256], bW_in[l][pr_ * 2 + half], defer)
                else:
                    for half in range(2):
                        src, c0 = w_in_view(w_in_d, half, blk)
                        wdma(src, W_in[l][:, :, c0:c0 + 512],
                             [bW_in[l][(2 * blk) * 2 + half], bW_in[l][(2 * blk + 1) * 2 + half]], defer)
                if blk == 0:
                    if l == 0:
                        wdma(a_w_r.rearrange("n i j -> i n j"), W_r[:], bW_r, defer)
                        wdma(a_w_i.rearrange("n i j -> i n j"), W_i[:], bW_i, defer)
                    else:
                        wdma(b_w_grp.rearrange("g (kc p) j -> p (g kc) j", p=128), W_grp[:], bW_grp, defer)
            wv = w_out_d.rearrange("(kc p) n -> p kc n", p=128)
            for q in range(2):
                wdma(wv[:, 4 * q:4 * q + 4, :], W_out[l][:, 4 * q:4 * q + 4, :], bW_out[l][q], defer)

        load_layer_weights(0, a_w_in, a_w_out, False)
        load_layer_weights(1, b_w_in, b_w_out, True)

        def maybe_casts():
            if wq:
                wq.pop(0)()

        class Tile:
            pass

        tiles = []
        for i in range(4):
            t = Tile()
            t.prompt, t.first, t.T, t.S, t.nsub, t.npart, t.t0, t.xb = True, i == 0, T, 1, 4, 128, i * T, i % 2
            tiles.append(t)
        t = Tile()
        t.prompt, t.first, t.T, t.S, t.nsub, t.npart, t.t0, t.xb = False, False, TS, NSEQ_S, 1, TS, 0, 0
        tiles.append(t)

        for b_ in range(2):
            for s_ in range(4):
                S.newsem("ldx%d_%d" % (b_, s_))
                S.newsem("stx%d_%d" % (b_, s_))
            S.newsem("stx%d_0a" % b_)

        def load_x_sub(tl, s):
            b = tl.xb
            if tl.prompt:
                src = x_p[tl.t0 + s * 128:tl.t0 + (s + 1) * 128, :]
                S.dma(lambda e: e.dma_start(out=XBt[b][:, s, :], in_=src), "ldx%d_%d" % (b, s), writes=[bXB[b][s]])
            elif s == 0:
                for st in range(STEPS):
                    S.dma(lambda e, st=st: e.dma_start(out=XBt[b][st * 16:(st + 1) * 16, 0, :], in_=x_s[:, st, :]),
                          "ldx%d_0" % b, writes=[bXB[b][0]])

        def load_x(tl, extra=()):
            for s in range(tl.nsub):
                load_x_sub(tl, s)

        def store_y_sub(tl, s):
            b = tl.xb
            if tl.prompt:
                dst = y_p[tl.t0 + s * 128:tl.t0 + (s + 1) * 128, :]
                S.dma(lambda e: e.dma_start(out=dst, in_=XBt[b][:, s, :]), "stx%d_%d" % (b, s), reads=[bXB[b][s]])
            else:
                for st in range(STEPS):
                    S.dma(lambda e, st=st: e.dma_start(out=y_s[:, st, :], in_=XBt[b][st * 16:(st + 1) * 16, 0, :]),
                          "stx%d_0%s" % (b, "a" if st % 2 else ""), reads=[bXB[b][0]], eng=("act" if st % 2 else "sp"))

        sscol = {"i": 0}
        rstd_of = {}

        def N1(tl, s, key):
            b, np_ = tl.xb, tl.npart
            col = sscol["i"] % 16
            sscol["i"] += 1
            X = XBt[b][0:np_, s, :]
            ssc = SSt[0:np_, col:col + 1]
            bSS = bSSc[col]
            junk = JUNK[0:np_, :]
            S.run("act", lambda e: e.activation(out=junk, in_=X, func=AF.Square, accum_out=ssc),
                  reads=[bXB[b][s]], writes=[bJUNK, bSS])
            S.run("act", lambda e: e.activation(out=ssc, in_=ssc, func=AF.Ln, scale=1.0 / D, bias=EPS),
                  reads=[bSS], writes=[bSS])
            S.run("act", lambda e: e.activation(out=ssc, in_=ssc, func=AF.Exp, scale=-0.5), reads=[bSS], writes=[bSS])
            rstd_of[key] = (ssc, bSS)

        def N1_batch(tl, lkey, subs=None):
            b, np_ = tl.xb, tl.npart
            subs = list(range(tl.nsub)) if subs is None else list(subs)
            ns = len(subs)
            g0 = (sscol["i"] + 3) // 4 * 4 % 16
            sscol["i"] = g0 + 4
            grp = SSt[0:np_, g0:g0 + ns]
            bgrp = [bSSc[g0 + i_] for i_ in range(ns)]
            for i_, s_ in enumerate(subs):
                X = XBt[b][0:np_, s_, :]
                ssc = SSt[0:np_, g0 + i_:g0 + i_ + 1]
                junk = JUNK[0:np_, :]
                S.run("act", lambda e, X=X, ssc=ssc, junk=junk: e.activation(out=junk, in_=X, func=AF.Square, accum_out=ssc),
                      reads=[bXB[b][s_]], writes=[bJUNK, bgrp[i_]])
            S.run("pool", lambda e: e.tensor_scalar(out=grp, in0=grp, scalar1=1.0 / D, scalar2=EPS, op0=ALU.mult, op1=ALU.add),
                  reads=bgrp, writes=bgrp)
            S.run("pool", lambda e: e.tensor_tensor(out=grp, in0=grp, in1=NEGH[0:np_, 0:ns], op=ALU.pow),
                  reads=bgrp + [bNEGH], writes=bgrp)
            for i_, s_ in enumerate(subs):
                rstd_of[(lkey, s_)] = (SSt[0:np_, g0 + i_:g0 + i_ + 1], bgrp[i_])

        n2_override = [None]

        def N2(tl, s, key):
            b, np_ = tl.xb, tl.npart
            ssc, bSS = rstd_of[key]
            X = XBt[b][0:np_, s, :]
            xs = XS[s % 2][0:np_, :]
            S.run(n2_override[0] or N2_ENG, lambda e: e.tensor_scalar(out=xs, in0=X, scalar1=ssc, scalar2=0.0, op0=ALU.mult, op1=ALU.add),
                  reads=[bXB[b][s], bSS], writes=[bXS[s % 2]])

        def N3(tl, s):
            np_ = tl.npart
            xs = XS[s % 2][0:np_, :]
            tb = s % 2
            tbv = PSB[tb][:, 0:NCH * np_].rearrange("p (k t) -> p k t", t=np_)

            def trf(e):
                last = None
                for k in range(NCH):
                    last = e.transpose(out=tbv[:, k, :], in_=xs[:, k * 128:(k + 1) * 128],
                                       identity=IDENTB[0:np_, 0:np_])
                return last
            S.run("pe", trf, reads=[bXS[s % 2], bIDB], writes=[bPS[tb]])

        def N4(tl, s, grow):
            np_ = tl.npart
            tb = s % 2
            tbv = PSB[tb][:, 0:NCH * np_].rearrange("p (k t) -> p k t", t=np_)
            gbc = VT[:, :, grow:grow + 1].to_broadcast([128, NCH, np_])
            S.run("dve", lambda e: e.tensor_tensor(out=UT[:, :, s * 128:s * 128 + np_], in0=tbv, in1=gbc, op=ALU.mult),
                  reads=[bPS[tb], bVT], writes=[bUTs[s]])

        def norm_pipeline(tl, lkey, n1_done=(), n2_done=()):
            ns = tl.nsub
            grow = V_ANORM if lkey == "l0" else V_BNORM
            for s in range(ns):
                if s not in n1_done:
                    N1(tl, s, (lkey, s))
                if s not in n2_done:
                    N2(tl, s, (lkey, s))
                N3(tl, s)
                if s >= 1:
                    N4(tl, s - 1, grow)
            N4(tl, ns - 1, grow)

        def inproj(l, tl, m, psi):
            Tt = tl.T

            def f(e):
                last = None
                for k in range(NCH):
                    last = e.matmul(PS[psi][:, 0:Tt], lhsT=W_in[l][:, k, m * 128:(m + 1) * 128], rhs=UT[:, k, 0:Tt],
                                    start=(k == 0), stop=(k == NCH - 1))
                return last
            widx = (m // 2) * 2 if m < NCH else ((m - NCH) // 2) * 2 + 1
            S.run("pe", f, reads=bUTs + [bW_in[l][widx]], writes=[bPS[psi]])

        NACC = {0: 6, 1: 6}

        PORD = {0: [0, 1, 2, 3], 1: [3, 2, 1, 0]}
        KORD = {l_: [c for j_ in PORD[l_] for c in (2 * j_, 2 * j_ + 1)] for l_ in (0, 1)}

        def out_part1(l, tl):
            np_ = tl.npart
            groups = [(s, half) for s in range(tl.nsub) for half in range(2)]
            groups = [g_ for g_ in groups if 2 * g_[0] + g_[1] < NACC[l]]

            def f(e):
                last = None
                for (s, half) in groups:
                    psi = 2 * s + half
                    for i_, k in enumerate(KORD[l][0:6]):
                        last = e.matmul(PS[psi][0:np_, :], lhsT=HGt[l][:, k, s * 128:s * 128 + np_],
                                        rhs=W_out[l][:, k, half * 512:(half + 1) * 512],
                                        start=(i_ == 0), stop=False)
                return last
            S.run("pe", f, reads=[bHG2[l][k] for k in KORD[l][0:6]] + bW_out[l],
                  writes=[bPS[2 * s + half] for (s, half) in groups])

        def out_part2(l, tl, s):
            b, np_ = tl.xb, tl.npart
            hs = l if tl.prompt else 1
            for half in range(2):
                psi = 2 * s + half
                k0 = 6 if (psi < NACC[l] and tl.opened) else 0

                def f(e, half=half, psi=psi, k0=k0):
                    last = None
                    for i_ in range(k0, NCH):
                        k = KORD[l][i_]
                        last = e.matmul(PS[psi][0:np_, :], lhsT=HGt[hs][:, k, s * 128:s * 128 + np_],
                                        rhs=W_out[l][:, k, half * 512:(half + 1) * 512],
                                        start=(i_ == 0), stop=(i_ == NCH - 1))
                    return last
                S.run("pe", f, reads=bHG2[hs] + bW_out[l], writes=[bPS[psi]])
                Xh = XBt[b][0:np_, s, half * 512:(half + 1) * 512]
                S.run("dve", lambda e, Xh=Xh, psi=psi: e.tensor_tensor(out=Xh, in0=PS[psi][0:np_, :], in1=Xh, op=ALU.add),
                      reads=[bPS[psi], bXB[b][s]], writes=[bXB[b][s]])

        def final_piece_a(tl, s):
            b, np_ = tl.xb, tl.npart
            col = sscol["i"] % 16
            sscol["i"] += 1
            X = XBt[b][0:np_, s, :]
            ssc = SSt[0:np_, col:col + 1]
            bSS = bSSc[col]
            junk = JUNK[0:np_, :]
            S.run("act", lambda e: e.activation(out=junk, in_=X, func=AF.Square, accum_out=ssc),
                  reads=[bXB[b][s]], writes=[bJUNK, bSS])
            S.run("pool", lambda e: e.tensor_scalar(out=ssc, in0=ssc, scalar1=1.0 / D, scalar2=EPS, op0=ALU.mult, op1=ALU.add),
                  reads=[bSS], writes=[bSS])
            S.run("pool", lambda e: e.tensor_tensor(out=ssc, in0=ssc, in1=NEGH[0:np_, 0:1], op=ALU.pow),
                  reads=[bSS, bNEGH], writes=[bSS])
            rstd_of[("fin", id(tl), s)] = (ssc, bSS)

        def final_piece_b(tl, s, next_tl):
            b, np_ = tl.xb, tl.npart
            ssc, bSS = rstd_of[("fin", id(tl), s)]
            X = XBt[b][0:np_, s, :]
            S.run("dve", lambda e: e.scalar_tensor_tensor(out=X, in0=X, scalar=ssc, in1=FN[0:np_, :], op0=ALU.mult,
                                                          op1=ALU.mult), reads=[bXB[b][s], bSS, bFN], writes=[bXB[b][s]])
            store_y_sub(tl, s)
            if next_tl is not None:
                load_x_sub(next_tl, s)

        def final_piece(tl, s, next_tl):
            final_piece_a(tl, s)
            final_piece_b(tl, s, next_tl)

        def A1pe_xb(l, tl, j):
            for ci, c in enumerate((2 * j, 2 * j + 1)):
                inproj(l, tl, c, 2 + 2 * ci)

        def A1pe_g(l, tl, j):
            for ci, c in enumerate((2 * j, 2 * j + 1)):
                inproj(l, tl, NCH + c, 3 + 2 * ci)

        def A1pe(l, tl, j):
            A1pe_xb(l, tl, j)
            A1pe_g(l, tl, j)

        def A1evac(l, tl, j):
            Tt, St, pr = tl.T, tl.S, tl.prompt
            HL = (3 if l == 0 else 15) * St
            Hh = (H0p if pr else H0s) if l == 0 else (H1p if pr else H1s)
            bHh = bH0[pr] if l == 0 else bH1[pr]
            for ci, c in enumerate((2 * j, 2 * j + 1)):
                ext = EXTt[ci]
                S.run("pool", lambda e, ext=ext, c=c: e.tensor_copy(out=ext[:, 0:HL], in_=Hh[:, c, :]),
                      reads=[bHh[c]], writes=[bEXT[ci]])
                if l == 0:
                    S.run("dve", lambda e, ext=ext, ci=ci: e.tensor_copy(out=ext[:, HL:HL + Tt], in_=PS[2 + 2 * ci][:, 0:Tt]),
                          reads=[bPS[2 + 2 * ci]], writes=[bEXT[ci]])
                else:
                    S.run("act", lambda e, ext=ext, ci=ci: e.activation(out=ext[:, HL:HL + Tt], in_=PS[2 + 2 * ci][:, 0:Tt],
                                                                         func=AF.Copy),
                          reads=[bPS[2 + 2 * ci]], writes=[bEXT[ci]])
                S.run("pool", lambda e, ext=ext, c=c: e.tensor_copy(out=Hh[:, c, :], in_=ext[:, Tt:Tt + HL]),
                      reads=[bEXT[ci]], writes=[bHh[c]])
                if l == 0:
                    conv_tap(tl, ci, c, 0)
            if l == 0:
                for k in range(1, 4):
                    for ci, c in enumerate((2 * j, 2 * j + 1)):
                        conv_tap(tl, ci, c, k)

        def A1silu(l, tl, j):
            Tt = tl.T
            for ci, c in enumerate((2 * j, 2 * j + 1)):
                q = (2 * j + ci) % 4
                S.run("act", lambda e, q=q, ci=ci: e.activation(out=SG[q][:, 0:Tt], in_=PS[3 + 2 * ci][:, 0:Tt],
                                                                 func=AF.Silu),
                      reads=[bPS[3 + 2 * ci]], writes=[bSG[q]])

        def conv_tap(tl, ci, c, k):
            Tt, St = tl.T, tl.S
            ext = EXTt[ci]
            xc = XC[ci][:, 0:Tt]
            if k == 0:
                S.run("dve", lambda e: e.tensor_scalar(
                    out=xc, in0=ext[:, 0:Tt], scalar1=vcol(c, V_CW0), scalar2=vcol(c, V_CB),
                    op0=ALU.mult, op1=ALU.add), reads=[bEXT[ci], bVT], writes=[bXC[ci]])
            else:
                S.run("dve", lambda e: e.scalar_tensor_tensor(
                    out=xc, in0=ext[:, k * St:k * St + Tt], scalar=vcol(c, V_CW0 + k), in1=xc,
                    op0=ALU.mult, op1=ALU.add), reads=[bEXT[ci], bVT, bXC[ci]], writes=[bXC[ci]])

        def conv(tl, ci, c):
            Tt, St = tl.T, tl.S
            ext = EXTt[ci]
            xc = XC[ci][:, 0:Tt]
            S.run("dve", lambda e: e.tensor_scalar(
                out=xc, in0=ext[:, 0:Tt], scalar1=vcol(c, V_CW0), scalar2=vcol(c, V_CB),
                op0=ALU.mult, op1=ALU.add), reads=[bEXT[ci], bVT], writes=[bXC[ci]])
            for k in range(1, 4):
                S.run("dve", lambda e, k=k: e.scalar_tensor_tensor(
                    out=xc, in0=ext[:, k * St:k * St + Tt], scalar=vcol(c, V_CW0 + k), in1=xc,
                    op0=ALU.mult, op1=ALU.add), reads=[bEXT[ci], bVT, bXC[ci]], writes=[bXC[ci]])

        def CASTRI(tl, j):
            Tt = tl.T
            for ci, c in enumerate((2 * j, 2 * j + 1)):
                xc = XC[ci][:, 0:Tt]
                S.run("act", lambda e, xc=xc, ci=ci: e.activation(out=XCBt[ci][:, 0:Tt], in_=xc, func=AF.Copy),
                      reads=[bXC[ci]], writes=[bXCB[ci]])
            maybe_casts()

        def TANHV(tl, j):
            Tt = tl.T
            for ci, c in enumerate((2 * j, 2 * j + 1)):
                xc = XC[ci][:, 0:Tt]

                def fri(e, c=c, ci=ci):
                    e.matmul(PS[6][:, 0:Tt], lhsT=W_r[:, c, :], rhs=XCBt[ci][:, 0:Tt], start=True, stop=True)
                    return e.matmul(PS[7][:, 0:Tt], lhsT=W_i[:, c, :], rhs=XCBt[ci][:, 0:Tt], start=True, stop=True)
                S.run("pe", fri, reads=[bXCB[ci], bW_r, bW_i], writes=[bPS[6], bPS[7]])
                S.run("act", lambda e, c=c, ci=ci: e.activation(out=TRt[ci][:, 0:Tt], in_=PS[6][:, 0:Tt], func=AF.Tanh,
                                                                 scale=0.5, bias=HBR[:, c:c + 1]),
                      reads=[bPS[6], bDER], writes=[bTR[ci]])
                S.run("act", lambda e, c=c, ci=ci: e.activation(out=V[ci][:, 0:Tt], in_=PS[7][:, 0:Tt], func=AF.Tanh,
                                                                 scale=0.5, bias=HBI[:, c:c + 1]),
                      reads=[bPS[7], bDER], writes=[bV[ci]])
                S.run("dve", lambda e, ci=ci, xc=xc: e.scalar_tensor_tensor(
                    out=V[ci][:, 0:Tt], in0=V[ci][:, 0:Tt], scalar=1.0, in1=xc, op0=ALU.add, op1=ALU.mult),
                    reads=[bV[ci], bXC[ci]], writes=[bV[ci]])

        def E0a(tl, j):
            Tt = tl.T
            chunks = (2 * j, 2 * j + 1)
            for ci, c in enumerate(chunks):
                tr = TRt[ci][:, 0:Tt]
                m = A2Mt[ci][:, 0:Tt]
                S.run("act", lambda e, tr=tr, m=m, c=c: e.activation(out=m, in_=tr, func=AF.Exp, scale=K2[:, c:c + 1],
                                                                      bias=K2[:, c:c + 1]),
                      reads=[bTR[ci], bDER], writes=[bA2M[ci]])
                S.run("act", lambda e, m=m: e.activation(out=m, in_=m, func=AF.Ln, scale=-1.0, bias=1.0),
                      reads=[bA2M[ci]], writes=[bA2M[ci]])
                S.run("act", lambda e, m=m: e.activation(out=m, in_=m, func=AF.Exp, scale=0.5, bias=LN_HALF),
                      reads=[bA2M[ci]], writes=[bA2M[ci]])
                if tl.first:
                    S.run("pool", lambda e, ci=ci: e.memset(A2Mt[ci][:, 0:1], 0.5), reads=[bA2M[ci]], writes=[bA2M[ci]])
                S.run("act", lambda e, tr=tr, c=c: e.activation(out=tr, in_=tr, func=AF.Exp, scale=K1[:, c:c + 1],
                                                                 bias=K1[:, c:c + 1]),
                      reads=[bTR[ci], bDER], writes=[bTR[ci]])

        def E0b(tl, j):
            Tt, St, pr = tl.T, tl.S, tl.prompt
            HST = HSTp if pr else HSTs
            for ci, c in enumerate((2 * j, 2 * j + 1)):
                q = (2 * j + ci) % 4
                a = TRt[ci][:, 0:Tt]
                m = A2Mt[ci][:, 0:Tt]
                bt = V[ci][:, 0:Tt]
                h = Ht[ci][:, 0:Tt]
                S.run("dve", lambda e, bt=bt, m=m: e.tensor_tensor(out=bt, in0=bt, in1=m, op=ALU.mult),
                      reads=[bV[ci], bA2M[ci]], writes=[bV[ci]])
                if pr:
                    S.run("dve", lambda e, h=h, a=a, bt=bt, c=c: e.tensor_tensor_scan(
                        out=h, data0=a, data1=bt, initial=HST[:, c, 0:1], op0=ALU.mult, op1=ALU.add),
                        reads=[bTR[ci], bV[ci], bHST[pr][c]], writes=[bH[ci]])
                else:
                    for st in range(STEPS):
                        sl = slice(st * St, (st + 1) * St)
                        prev = HST[:, c, :] if st == 0 else Ht[ci][:, (st - 1) * St:st * St]
                        S.run("dve", lambda e, ci=ci, sl=sl, prev=prev: e.tensor_tensor(
                            out=Ht[ci][:, sl], in0=TRt[ci][:, sl], in1=prev, op=ALU.mult),
                            reads=[bTR[ci], bHST[pr][c], bH[ci]], writes=[bH[ci]])
                        S.run("dve", lambda e, ci=ci, sl=sl: e.tensor_tensor(
                            out=Ht[ci][:, sl], in0=Ht[ci][:, sl], in1=V[ci][:, sl], op=ALU.add),
                            reads=[bV[ci], bH[ci]], writes=[bH[ci]])
                S.run("pool", lambda e, h=h, c=c: e.tensor_copy(out=HST[:, c, :], in_=h[:, Tt - St:Tt]),
                      reads=[bH[ci]], writes=[bHST[pr][c]])
                S.run("dve", lambda e, h=h, c=c, q=q: e.tensor_tensor(
                    out=HG[:, c, 0:Tt], in0=h, in1=SG[q][:, 0:Tt], op=ALU.mult),
                    reads=[bH[ci], bSG[q]], writes=[bHGc[c]])
                maybe_casts()

        PTW = [TMP[:, 2048 + i * EXTW:2048 + (i + 1) * EXTW] for i in range(2)]
        bPTW = [Buf("ptw%d" % i) for i in range(2)]

        TRB = [t_[:].bitcast(BF16) for t_ in TRt]

        def pooled_buf(j, ci, Tt):
            if PORD[1].index(j) % 2 == 0:
                return XCBt[ci][:, 0:Tt], bXCB[ci]
            return TRB[ci][:, 0:Tt], bTR[ci]

        def P1(tl, j, only=None):
            Tt, St = tl.T, tl.S
            HL = 15 * St
            Wd = POOL_W[j]
            for ci, c in enumerate((2 * j, 2 * j + 1)):
                if only is not None and ci != only:
                    continue
                ext = EXTt[ci]
                pa, pb = PTW[0], PTW[1]
                bpa, bpb = bPTW[0], bPTW[1]
                src, bsrc = ext, bEXT[ci]
                lvl = 1
                dst, bdst = pa, bpa
                while lvl < Wd:
                    lo = HL - (Wd - 2 * lvl) * St
                    hi = HL + Tt
                    sh = lvl * St
                    S.run("dve", lambda e, dst=dst, src=src, lo=lo, hi=hi, sh=sh: e.tensor_tensor(
                        out=dst[:, lo:hi], in0=src[:, lo:hi], in1=src[:, lo - sh:hi - sh], op=ALU.add),
                        reads=[bsrc], writes=[bdst])
                    src, bsrc = dst, bdst
                    dst, bdst = (pb, bpb) if dst is pa else (pa, bpa)
                    lvl *= 2
                pooled, bpooled = pooled_buf(j, ci, Tt)
                S.run("dve", lambda e, src=src, ext=ext, pooled=pooled, Wd=Wd: e.scalar_tensor_tensor(
                    out=pooled, in0=src[:, HL:HL + Tt], scalar=1.0 / Wd, in1=ext[:, HL:HL + Tt],
                    op0=ALU.mult, op1=ALU.subtract), reads=[bsrc, bEXT[ci]], writes=[bpooled])
                if tl.first:
                    rc_group(j)
                    tmp = TMP16[:, 0:16]
                    S.run("dve", lambda e, src=src, tmp=tmp, j=j: e.tensor_tensor(out=tmp, in0=src[:, HL:HL + 16],
                                                                                   in1=RC[:, j, :], op=ALU.mult),
                          reads=[bsrc, bRC], writes=[bT16])
                    S.run("dve", lambda e, ext=ext, tmp=tmp, pooled=pooled: e.tensor_tensor(
                        out=pooled[:, 0:16], in0=tmp, in1=ext[:, HL:HL + 16], op=ALU.subtract),
                        reads=[bT16, bEXT[ci]], writes=[bpooled])

        def Z1pe(tl, j):
            Tt = tl.T
            pl = [pooled_buf(j, kc, Tt) for kc in range(2)]
            for mi in range(2):
                def fz(e, mi=mi, j=j):
                    last = None
                    for kc in range(2):
                        last = e.matmul(PS[6 + mi][:, 0:Tt], lhsT=W_grp[:, 2 * j + kc, mi * 128:(mi + 1) * 128],
                                        rhs=pl[kc][0], start=(kc == 0), stop=(kc == 1))
                    return last
                S.run("pe", fz, reads=[pl[0][1], pl[1][1], bW_grp], writes=[bPS[6 + mi]])

        def Z1act(tl, j):
            Tt = tl.T
            for mi, c in enumerate((2 * j, 2 * j + 1)):
                zt = A2Mt[mi][:, 0:Tt]
                S.run("act", lambda e, mi=mi, c=c, zt=zt: e.activation(out=zt, in_=PS[6 + mi][:, 0:Tt], func=AF.Identity,
                                                                        scale=vcol(c, V_BSCALE), bias=BGS[:, c:c + 1]),
                      reads=[bPS[6 + mi], bVT, bDER], writes=[bA2M[mi]])

        def Z1dve(tl, j):
            Tt = tl.T
            for mi, c in enumerate((2 * j, 2 * j + 1)):
                q = (2 * j + mi) % 4
                zt = A2Mt[mi][:, 0:Tt]
                S.run("dve", lambda e, c=c, q=q, zt=zt: e.tensor_tensor(out=HGt[1][:, c, 0:Tt], in0=zt, in1=SG[q][:, 0:Tt],
                                                                        op=ALU.mult),
                      reads=[bA2M[mi], bSG[q]], writes=[bHG2[1][c]])

        alias_bufs = [bV[0], bV[1], bXC[0], bXC[1]]

        pending_final = []

        def flush_final(n=99):
            while pending_final and n > 0:
                final_piece(*pending_final.pop(0))
                n -= 1

        drainq = []

        def drain1(n=1):
            while drainq and n > 0:
                t_, s_, half = drainq.pop(0)
                psi = half
                np_ = t_.npart

                def f(e, t_=t_, s_=s_, half=half, psi=psi, np_=np_):
                    last = None
                    for k in range(NCH):
                        last = e.matmul(PS[psi][0:np_, :], lhsT=HGt[1][:, k, s_ * 128:s_ * 128 + np_],
                                        rhs=W_out[1][:, k, half * 512:(half + 1) * 512], start=(k == 0), stop=(k == NCH - 1))
                    return last
                S.run("pe", f, reads=bHG2[1] + bW_out[1], writes=[bPS[psi]])
                Xh = XBt[t_.xb][0:np_, s_, half * 512:(half + 1) * 512]
                S.run("dve", lambda e, Xh=Xh, psi=psi, np_=np_: e.tensor_tensor(out=Xh, in0=PS[psi][0:np_, :], in1=Xh, op=ALU.add),
                      reads=[bPS[psi], bXB[t_.xb][s_]], writes=[bXB[t_.xb][s_]])
                n -= 1

        def layer0(tl, next_tl):
            A1pe(0, tl, 0)
            A1evac(0, tl, 0)
            A1silu(0, tl, 0)
            CASTRI(tl, 0)
            TANHV(tl, 0)
            A1pe(0, tl, 1)
            for j in range(4):
                if j + 1 <= 3:
                    A1evac(0, tl, j + 1)
                drain1()
                if j == 3:
                    drain1(99)
                    out_part1(0, tl)
                E0a(tl, j)
                drain1()
                if j + 1 <= 3:
                    A1silu(0, tl, j + 1)
                    CASTRI(tl, j + 1)
                drain1()
                E0b(tl, j)
                drain1()
                if j + 1 <= 3:
                    TANHV(tl, j + 1)
                if j + 2 <= 3:
                    A1pe(0, tl, j + 2)

        def layer1(tl, next_tl):
            guard = []
            for bb in alias_bufs:
                guard.append(bb.w)
                guard.extend(bb.readers())
            po = PORD[1]
            A1pe(1, tl, po[0])
            A1evac(1, tl, po[0])
            A1silu(1, tl, po[0])
            S.wait_only("dve", guard)
            P1(tl, po[0])
            for i in range(4):
                j = po[i]
                jn = po[i + 1] if i + 1 <= 3 else None
                if i == 1:
                    flush_final(2)
                if i == 3 and next_tl is not None and next_tl.nsub == 4:
                    N1_batch(next_tl, "l0", subs=[2, 3])
                if jn is not None:
                    A1pe_xb(1, tl, jn)
                if i == 2 and next_tl is not None and not next_tl.prompt:
                    load_pool_rows()
                if i == 3 and not tl.defer1:
                    out_part1(1, tl)
                Z1pe(tl, j)
                if jn is not None:
                    A1pe_g(1, tl, jn)
                    A1evac(1, tl, jn)
                Z1act(tl, j)
                if jn is not None:
                    A1silu(1, tl, jn)
                    P1(tl, jn, only=0)
                Z1dve(tl, j)
                if jn is not None:
                    P1(tl, jn, only=1)
                if i == 3 and next_tl is not None and not next_tl.prompt:
                    pool_state_transposes("dve", (6, 7))
                flush_final({0: 2, 1: 0, 2: 0, 3: 0}[i])
                if i == 2 and next_tl is not None:
                    N1_batch(next_tl, "l0", subs=range(min(2, next_tl.nsub)))
                    for s in range(min(2, next_tl.nsub)):
                        N2(next_tl, s, ("l0", s))
            lastdve = Ev("E_dve", S.semcnt["E_dve"])
            for bb in alias_bufs:
                bb.add_read(lastdve)

        def bc(row_ap, w):
            return row_ap.unsqueeze(2).to_broadcast([128, NCH, w])

        def emit_rows(src_fn, bsrc, nrows, psa, psb, stage, bstage, dmas, sem, all_act=False, all_dve=False):
            for half, psi in ((0, psa), (1, psb)):
                def trf(e, half=half, psi=psi):
                    last = None
                    for cc in range(4):
                        c = half * 4 + cc
                        last = e.transpose(out=PS[psi][0:nrows, cc * 128:(cc + 1) * 128], in_=src_fn(c), identity=IDENT[:, :])
                    return last
                S.run("pe", trf, reads=[bsrc[half * 4 + cc] for cc in range(4)] + [bID], writes=[bPS[psi]])
                if all_dve or (half == 0 and not all_act):
                    S.run("dve", lambda e, psi=psi, half=half: e.tensor_copy(out=stage[0:nrows, half * 512:(half + 1) * 512],
                                                                             in_=PS[psi][0:nrows, :]),
                          reads=[bPS[psi]], writes=bstage)
                else:
                    S.run("act", lambda e, psi=psi, half=half: e.activation(
                        out=stage[0:nrows, half * 512:(half + 1) * 512], in_=PS[psi][0:nrows, :], func=AF.Copy),
                        reads=[bPS[psi]], writes=bstage)
            for (dst, r0, n) in dmas:
                S.dma(lambda e, dst=dst, r0=r0, n=n: e.dma_start(out=dst, in_=stage[r0:r0 + n, :]), sem, reads=bstage)

        SRO = [XBt[0][:, 1 + i, :] for i in range(3)]
        bSRO = [[bXB[0][1 + i]] for i in range(3)]

        def sample_layers(tl, which, prev_tl=None, hook=None, hook2=None):
            Tt, St = TS, NSEQ_S
            c8 = lambda ap, w: ap.rearrange("p (c t) -> p c t", t=w)
            allb = bSG + bV + bXC + bTR + bA2M + bH + bXCB + bEXT + bPTW
            guard = []
            for bb in allb:
                guard.append(bb.w)
                guard.extend(bb.readers())
            bs = {k: Buf("s_" + k) for k in ("ext", "sg", "v", "xc", "xcb", "tr", "m", "h", "pt0", "pt1", "zt", "pl", "t2")}

            def run(eng, fn, reads=(), writes=()):
                return S.run(eng, fn, reads=reads, writes=writes, deps=guard)

            xbps, gps = c8(PS[2][:, :], Tt), c8(PS[3][:, :], Tt)
            rps, ips = c8(PS[6][:, :], Tt), c8(PS[7][:, :], Tt)

            def inproj_all(l):
                for half, psi in ((0, 2), (1, 3)):
                    def f(e, half=half, psi=psi):
                        last = None
                        for c in range(NCH):
                            m = half * NCH + c
                            for k in range(NCH):
                                last = e.matmul(PS[psi][:, c * Tt:(c + 1) * Tt], lhsT=W_in[l][:, k, m * 128:(m + 1) * 128],
                                                rhs=UT[:, k, 0:Tt], start=(k == 0), stop=(k == NCH - 1))
                        return last
                    run("pe", f, reads=bUTs + bW_in[l], writes=[bPS[psi]])

            def act_warm(func):
                S.run("act", lambda e: e.activation(out=JUNK[:, 0:4], in_=NEGH[:, 0:4], func=func), reads=[bNEGH], writes=[bJUNK])

            if which == 0:
                HL = 3 * St
                EW = HL + Tt
                ext = c8(TMP[:, 0:NCH * EW], EW)
                xc = c8(TMP[:, 1024:1536], Tt)
                v = c8(TMP[:, 1536:2048], Tt)
                sg = c8(TMP[:, 2048:2560], Tt)
                t2 = c8(TMP[:, 2560:3072], Tt)
                tr, m, h = c8(TRt[0][:, :], Tt), c8(A2Mt[0][:, :], Tt), c8(Ht[0][:, :], Tt)
                xcb = c8(XCBt[0][:, :], Tt)
                inproj_all(0)
                run("act", lambda e: e.activation(out=ext[:, :, 0:HL], in_=H0s[:, :, :], func=AF.Copy), reads=bH0[False],
                    writes=[bs["ext"]])
                run("dve", lambda e: e.tensor_copy(out=ext[:, :, HL:EW], in_=xbps), reads=[bPS[2]], writes=[bs["ext"]])
                fpa = list(range(prev_tl.nsub)) if prev_tl is not None else []
                for _ in range(2):
                    if fpa:
                        final_piece_a(prev_tl, fpa.pop(0))
                run("act", lambda e: e.activation(out=sg, in_=gps, func=AF.Silu), reads=[bPS[3]], writes=[bs["sg"]])
                run("act", lambda e: e.activation(out=H0s[:, :, :], in_=ext[:, :, Tt:EW], func=AF.Copy), reads=[bs["ext"]],
                    writes=bH0[False])
                if fpa:
                    final_piece_a(prev_tl, fpa.pop(0))
                if hook is not None:
                    hook()
                run("dve", lambda e: e.tensor_tensor(out=xc, in0=ext[:, :, 0:Tt], in1=bc(VT[:, :, V_CW0], Tt), op=ALU.mult),
                    reads=[bs["ext"], bVT], writes=[bs["xc"]])
                for k in range(1, 4):
                    run("dve", lambda e, k=k: e.tensor_tensor(out=t2, in0=ext[:, :, k * St:k * St + Tt],
                                                              in1=bc(VT[:, :, V_CW0 + k], Tt), op=ALU.mult),
                        reads=[bs["ext"], bVT], writes=[bs["t2"]])
                    run("dve", lambda e: e.tensor_tensor(out=xc, in0=xc, in1=t2, op=ALU.add),
                        reads=[bs["xc"], bs["t2"]], writes=[bs["xc"]])
                run("dve", lambda e: e.tensor_tensor(out=xcb, in0=xc, in1=bc(VT[:, :, V_CB], Tt), op=ALU.add),
                    reads=[bs["xc"], bVT], writes=[bs["xcb"]])
                run("dve", lambda e: e.tensor_tensor(out=xc, in0=xc, in1=bc(VT[:, :, V_CB], Tt), op=ALU.add),
                    reads=[bs["xc"], bVT], writes=[bs["xc"]])
                while fpa:
                    final_piece_a(prev_tl, fpa.pop(0))
                fpb = list(range(prev_tl.nsub)) if prev_tl is not None else []

                def fri(e):
                    last = None
                    for c in range(NCH):
                        e.matmul(PS[6][:, c * Tt:(c + 1) * Tt], lhsT=W_r[:, c, :], rhs=xcb[:, c, :], start=True, stop=True)
                        last = e.matmul(PS[7][:, c * Tt:(c + 1) * Tt], lhsT=W_i[:, c, :], rhs=xcb[:, c, :], start=True, stop=True)
                    return last
                run("pe", fri, reads=[bs["xcb"], bW_r, bW_i], writes=[bPS[6], bPS[7]])
                run("dve", lambda e: e.scalar_tensor_tensor(out=tr, in0=rps, scalar=0.5, in1=bc(HBR, Tt), op0=ALU.mult,
                                                            op1=ALU.add), reads=[bPS[6], bDER], writes=[bs["tr"]])
                run("dve", lambda e: e.scalar_tensor_tensor(out=v, in0=ips, scalar=0.5, in1=bc(HBI, Tt), op0=ALU.mult,
                                                            op1=ALU.add), reads=[bPS[7], bDER], writes=[bs["v"]])
                run("act", lambda e: e.activation(out=tr, in_=tr, func=AF.Tanh), reads=[bs["tr"]], writes=[bs["tr"]])
                run("act", lambda e: e.activation(out=v, in_=v, func=AF.Tanh), reads=[bs["v"]], writes=[bs["v"]])
                act_warm(AF.Exp)
                run("dve", lambda e: e.scalar_tensor_tensor(out=v, in0=v, scalar=1.0, in1=xc, op0=ALU.add, op1=ALU.mult),
                    reads=[bs["v"], bs["xc"]], writes=[bs["v"]])
                run("dve", lambda e: e.scalar_tensor_tensor(out=tr, in0=tr, scalar=1.0, in1=bc(K1, Tt), op0=ALU.add,
                                                            op1=ALU.mult), reads=[bs["tr"], bDER], writes=[bs["tr"]])
                run("act", lambda e: e.activation(out=m, in_=tr, func=AF.Exp, scale=2.0), reads=[bs["tr"]], writes=[bs["m"]])
                run("act", lambda e: e.activation(out=m, in_=m, func=AF.Ln, scale=-1.0, bias=1.0), reads=[bs["m"]], writes=[bs["m"]])
                run("act", lambda e: e.activation(out=m, in_=m, func=AF.Exp, scale=0.5, bias=LN_HALF), reads=[bs["m"]],
                    writes=[bs["m"]])
                run("act", lambda e: e.activation(out=tr, in_=tr, func=AF.Exp), reads=[bs["tr"]], writes=[bs["tr"]])
                for _ in range(2):
                    if fpb:
                        final_piece_b(prev_tl, fpb.pop(0), None)
                run("dve", lambda e: e.tensor_tensor(out=v, in0=v, in1=m, op=ALU.mult), reads=[bs["v"], bs["m"]], writes=[bs["v"]])
                for st in range(STEPS):
                    sl = slice(st * St, (st + 1) * St)
                    prev = HSTs[:, :, :] if st == 0 else h[:, :, (st - 1) * St:st * St]
                    run("dve", lambda e, sl=sl, prev=prev: e.tensor_tensor(out=h[:, :, sl], in0=tr[:, :, sl], in1=prev, op=ALU.mult),
                        reads=[bs["tr"], bs["h"]] + bHST[False], writes=[bs["h"]])
                    run("dve", lambda e, sl=sl: e.tensor_tensor(out=h[:, :, sl], in0=h[:, :, sl], in1=v[:, :, sl], op=ALU.add),
                        reads=[bs["v"], bs["h"]], writes=[bs["h"]])
                run("act", lambda e: e.activation(out=HSTs[:, :, :], in_=h[:, :, Tt - St:Tt], func=AF.Copy), reads=[bs["h"]],
                    writes=bHST[False])
                run("dve", lambda e: e.tensor_tensor(out=HGt[1][:, :, 0:Tt], in0=h, in1=sg, op=ALU.mult),
                    reads=[bs["h"], bs["sg"]], writes=bHG2[1])
                while fpb:
                    final_piece_b(prev_tl, fpb.pop(0), None)
            else:
                HL = 15 * St
                EW = HL + Tt
                ext = c8(TMP[:, 0:NCH * EW], EW)
                pt = [c8(TMP[:, 2432:2432 + 2 * EW], EW), c8(TMP[:, 2432 + 2 * EW:2432 + 4 * EW], EW)]
                sg, zt = c8(Ht[0][:, :], Tt), c8(Ht[1][:, :], Tt)
                pl = c8(XCBt[0][:, :], Tt)
                inproj_all(1)
                act_warm(AF.Silu)
                run("act", lambda e: e.activation(out=ext[:, :, 0:HL], in_=H1s[:, :, :], func=AF.Copy), reads=bH1[False],
                    writes=[bs["ext"]])
                run("dve", lambda e: e.tensor_copy(out=ext[:, :, HL:EW], in_=xbps), reads=[bPS[2]], writes=[bs["ext"]])
                run("act", lambda e: e.activation(out=sg, in_=gps, func=AF.Silu), reads=[bPS[3]], writes=[bs["sg"]])
                run("act", lambda e: e.activation(out=H1s[:, :, :], in_=ext[:, :, Tt:EW], func=AF.Copy), reads=[bs["ext"]],
                    writes=bH1[False])
                if hook is not None:
                    hook()
                for g, Wd in enumerate(POOL_W):
                    cs = slice(2 * g, 2 * g + 2)
                    src = ext[:, cs, :]
                    bsrc = bs["ext"]
                    lvl, di = 1, 0
                    while lvl < Wd:
                        lo = HL - (Wd - 2 * lvl) * St
                        sh = lvl * St
                        dst = pt[di]
                        run("dve", lambda e, dst=dst, src=src, lo=lo, sh=sh: e.tensor_tensor(
                            out=dst[:, :, lo:EW], in0=src[:, :, lo:EW], in1=src[:, :, lo - sh:EW - sh], op=ALU.add),
                            reads=[bsrc], writes=[bs["pt%d" % di]])
                        src, bsrc = dst, bs["pt%d" % di]
                        di ^= 1
                        lvl *= 2
                    run("dve", lambda e, src=src, cs=cs, Wd=Wd: e.scalar_tensor_tensor(
                        out=pl[:, cs, :], in0=src[:, :, HL:EW], scalar=1.0 / Wd, in1=ext[:, cs, HL:EW],
                        op0=ALU.mult, op1=ALU.subtract), reads=[bsrc, bs["ext"]], writes=[bs["pl"]])

                def fz(e):
                    last = None
                    for g in range(4):
                        for mi in range(2):
                            c = 2 * g + mi
                            for kc in range(2):
                                last = e.matmul(PS[6][:, c * Tt:(c + 1) * Tt], lhsT=W_grp[:, 2 * g + kc, mi * 128:(mi + 1) * 128],
                                                rhs=pl[:, 2 * g + kc, :], start=(kc == 0), stop=(kc == 1))
                    return last
                run("pe", fz, reads=[bs["pl"], bW_grp], writes=[bPS[6]])
                if hook2 is not None:
                    hook2()
                run("dve", lambda e: e.tensor_tensor(out=zt, in0=rps, in1=bc(VT[:, :, V_BSCALE], Tt), op=ALU.mult),
                    reads=[bPS[6], bVT], writes=[bs["zt"]])
                run("dve", lambda e: e.tensor_tensor(out=zt, in0=zt, in1=bc(BGS, Tt), op=ALU.add),
                    reads=[bs["zt"], bDER], writes=[bs["zt"]])
                run("dve", lambda e: e.tensor_tensor(out=HGt[1][:, :, 0:Tt], in0=zt, in1=sg, op=ALU.mult),
                    reads=[bs["zt"], bs["sg"]], writes=bHG2[1])
            for bb in bs.values():
                evs = [bb.w] + bb.readers()
                for tgt in allb:
                    for ev in evs:
                        tgt.add_read(ev)

        SRX = [XBt[1][:, i, :] for i in range(4)]
        for j in range(3):
            S.dma(lambda e, j=j: e.dma_start(out=SRX[0][j * 16:(j + 1) * 16, :], in_=st_conv[:, j, :]), "ld_sr",
                  writes=bXB[1])
        S.dma(lambda e: e.dma_start(out=SRX[1][0:16, :], in_=st_lru), "ld_sr", writes=bXB[1])
        S.newsem("ld_pool")

        def load_pool_rows():
            gate = []
            for bb in (bXB[0][2], bXB[0][3]):
                gate.append(bb.w)
                gate.extend(bb.readers())
            ev = None
            for j in range(15):
                dst = XBt[0][:, 2, :] if j < 8 else XBt[0][:, 3, :]
                jj = j if j < 8 else j - 8
                ev = S.dma(lambda e, j=j, jj=jj, dst=dst: e.dma_start(out=dst[jj * 16:(jj + 1) * 16, :], in_=st_pool[:, j, :]),
                           "ld_pool", deps=gate)
            for bb in (bXB[0][2], bXB[0][3]):
                bb.w = ev
                bb.r = {}

        def rows_to_fm4(src, bsrc, nrows, dst4_fn, bdst_all, psis, eng):
            for half in range(2):
                psi = psis[half]

                def trf(e, half=half, psi=psi):
                    last = None
                    for cc in range(4):
                        c = half * 4 + cc
                        last = e.transpose(out=PS[psi][:, cc * 128:cc * 128 + nrows],
                                           in_=src[0:nrows, c * 128:(c + 1) * 128],
                                           identity=IDENT[0:nrows, 0:nrows])
                    return last
                S.run("pe", trf, reads=list(bsrc) + [bID], writes=[bPS[psi]])
                pv = PS[psi][:, :].rearrange("p (c r) -> p c r", r=128)[:, :, 0:nrows]
                if eng == "dve":
                    S.run("dve", lambda e, half=half, pv=pv: e.tensor_copy(out=dst4_fn(half), in_=pv),
                          reads=[bPS[psi]], writes=bdst_all)
                else:
                    S.run("act", lambda e, half=half, pv=pv: e.activation(out=dst4_fn(half), in_=pv, func=AF.Copy),
                          reads=[bPS[psi]], writes=bdst_all)

        def pool_state_transposes(eng="dve", psis=(0, 1)):
            bH1all = bH1[False] + bHG2[0]
            rows_to_fm4(XBt[0][:, 2, :], [bXB[0][2], bXB[0][3]], 128, lambda h: H1s[:, 4 * h:4 * h + 4, 0:128], bH1all, psis, eng)
            rows_to_fm4(XBt[0][:, 3, :], [bXB[0][2], bXB[0][3]], 112, lambda h: H1s[:, 4 * h:4 * h + 4, 128:240], bH1all, psis, eng)

        def state_transposes():
            rows_to_fm(SRX[0], bXB[1], 48, lambda c: H0s[:, c, :], bH0[False], 6)
            rows_to_fm(SRX[1], bXB[1], 16, lambda c: HSTs[:, c, :], bHST[False], 7)

        n2_override[0] = "dve"
        norm_pipeline(tiles[0], "l0")
        n2_override[0] = None
        late_prologue()
        prompt_states_pending = [False]
        for ti, tl in enumerate(tiles):
            nxt = tiles[ti + 1] if ti + 1 < len(tiles) else None
            tl.opened = tl.prompt
            tl.defer1 = tl.prompt and nxt is not None and nxt.prompt
            if tl.prompt:
                layer0(tl, nxt)
            else:
                prev_t = pending_final[0][0] if pending_final else None
                del pending_final[:]
                sample_layers(tl, 0, prev_t)
            while wq:
                maybe_casts()
            for s in range(tl.nsub):
                out_part2(0, tl, s)
            if not tl.prompt:
                S.newsem("st_m7")
                emit_rows(lambda c: H0s[:, c, :], bH0[False], 48, 4, 5, SRO[0], bSRO[0],
                          [(o_conv_s[:, j, :], j * 16, 16) for j in range(3)], "st_m3", all_dve=True)
                emit_rows(lambda c: HSTs[:, c, :], bHST[False], 16, 4, 5, SRO[1], bSRO[1], [(o_lru_s, 0, 16)], "st_m7", all_dve=True)
            norm_pipeline(tl, "l1")
            if ti == 0:
                state_transposes()
                load_x(tiles[1])
            if tl.prompt:
                layer1(tl, nxt)
            else:
                for k_ in ("st_m4", "st_m5", "st_m6"):
                    S.newsem(k_)

                def emit_prompt_states():
                    emit_rows(lambda c: H0p[:, c, :], bH0[True], 3, 4, 5, SRO[1], bSRO[1], [(o_conv_p, 0, 3)], "st_m4", all_act=True)
                    emit_rows(lambda c: HSTp[:, c, :], bHST[True], 1, 4, 5, SRO[2], bSRO[2], [(o_lru_p, 0, 1)], "st_m5", all_act=True)

                    emit_rows(lambda c: H1p[:, c, :], bH1[True], 15, 4, 5, SRO[0], bSRO[0], [(o_pool_p, 0, 15)], "st_m6", all_act=True)

                def emit_states2():
                    emit_rows(lambda c: H1s[:, c, 0:128], bH1[False], 128, 4, 5, SRO[1], bSRO[1],
                              [(o_pool_s[:, j, :], j * 16, 16) for j in range(8)], "st_m0", all_act=True)
                    emit_rows(lambda c: H1s[:, c, 128:240], bH1[False], 112, 4, 5, SRO[2], bSRO[2],
                              [(o_pool_s[:, j, :], (j - 8) * 16, 16) for j in range(8, 15)], "st_m2", all_act=True)
                sample_layers(tl, 1, hook=emit_prompt_states, hook2=emit_states2)
            if tl.defer1:
                for s in range(tl.nsub):
                    for half in range(2):
                        drainq.append((tl, s, half))
            else:
                for s in range(tl.nsub):
                    out_part2(1, tl, s)
            if False:
                pass
            if tl.prompt and nxt is not None and not nxt.prompt:
                prompt_states_pending[0] = True
            if nxt is not None:
                ns = nxt.nsub
                norm_pipeline(nxt, "l0", n1_done=range(ns), n2_done=range(min(2, ns)))
            nn = tiles[ti + 2] if ti + 2 < len(tiles) else None
            for s_ in range(tl.nsub):
                pending_final.append((tl, s_, nn))
            if nxt is None:
                flush_final()

        fin = [Ev(k, S.semcnt[k]) for k in S.semcnt if (k.startswith("stx") or k.startswith("st_m")) and S.semcnt[k] > 0]
        S.wait_only("sp", fin)
        S.emit()
    return nc


_VEC_ROWS = None


def kernel(**inputs):
    f32 = np.float32
    g = {k: np.asarray(v) for k, v in inputs.items()}
    vecs = np.zeros((16, D), f32)
    vecs[V_ANORM] = g["a_norm"][0]
    vecs[V_CW0:V_CW0 + 4] = g["a_conv_w"][0]
    vecs[V_CB] = g["a_conv_b"][0]
    vecs[V_BR] = g["a_b_r"][0]
    vecs[V_BI] = g["a_b_i"][0]
    vecs[V_LAM] = g["a_lam"][0]
    vecs[V_BNORM] = g["b_norm"][0]
    vecs[V_BGRP] = g["b_b_grp"][0]
    vecs[V_BSCALE] = g["b_scale"][0]
    shared = {
        "vecs": vecs,
        "fnorm": np.ascontiguousarray(g["final_norm"].reshape(1, D).astype(f32)),
        "a_w_in": np.ascontiguousarray(g["a_w_in"][0], f32),
        "a_w_r": np.ascontiguousarray(g["a_w_r"][0], f32),
        "a_w_i": np.ascontiguousarray(g["a_w_i"][0], f32),
        "a_w_out": np.ascontiguousarray(g["a_w_out"][0], f32),
        "b_w_in": np.ascontiguousarray(g["b_w_in"][0], f32),
        "b_w_grp": np.ascontiguousarray(g["b_w_grp"][0], f32),
        "b_w_out": np.ascontiguousarray(g["b_w_out"][0], f32),
    }
    in_maps = []
    for c in range(NCORES):
        sl = slice(c * NSEQ_S, (c + 1) * NSEQ_S)
        m = dict(shared)
        m["x_p"] = np.ascontiguousarray(g["x_prompt"][c], f32)
        m["x_s"] = np.ascontiguousarray(g["x_sample"][sl], f32)
        m["st_conv"] = np.ascontiguousarray(g["state_conv"][0, sl], f32)
        m["st_lru"] = np.ascontiguousarray(g["state_lru"][0, sl], f32)
        m["st_pool"] = np.ascontiguousarray(g["state_pool"][0, sl], f32)
        in_maps.append(m)
    nc = build_program()
    res = run_bass_kernel_spmd(nc, in_maps, core_ids=list(range(NCORES)))
    R = res.results
    y_prompt = np.stack([R[c]["y_p"] for c in range(NCORES)], 0)
    y_sample = np.concatenate([R[c]["y_s"] for c in range(NCORES)], 0)
    conv_p = np.stack([R[c]["o_conv_p"] for c in range(NCORES)], 0)[None]
    lru_p = np.concatenate([R[c]["o_lru_p"] for c in range(NCORES)], 0)[None]
    pool_p = np.stack([R[c]["o_pool_p"] for c in range(NCORES)], 0)[None]
    conv_s = np.concatenate([R[c]["o_conv_s"] for c in range(NCORES)], 0)[None]
    lru_s = np.concatenate([R[c]["o_lru_s"] for c in range(NCORES)], 0)[None]
    pool_s = np.concatenate([R[c]["o_pool_s"] for c in range(NCORES)], 0)[None]
    return tuple(np.ascontiguousarray(a.astype(f32)) for a in
                 (y_prompt, y_sample, conv_p, lru_p, pool_p, conv_s, lru_s, pool_s))
```

```python
import numpy as np
from contextlib import ExitStack
import concourse.bass as bass
import concourse.mybir as mybir
from concourse.bass_utils import run_bass_kernel_spmd

F32 = mybir.dt.float32
BF16 = mybir.dt.bfloat16
I32 = mybir.dt.int32
AF = mybir.ActivationFunctionType
ALU = mybir.AluOpType

NCORES = 8
D = 1024
NCH = 8
SEQ = 2048
T = 512
NSEQ_S = 16
STEPS = 4
TS = NSEQ_S * STEPS
EPS = 1e-6
POOL_W = (2, 4, 8, 16)
LN_HALF = float(np.log(0.5))

V_ANORM, V_CW0, V_CB, V_BR, V_BI, V_LAM, V_BNORM, V_BGRP, V_BSCALE = 0, 1, 5, 6, 7, 8, 9, 10, 11


class Ev:
    __slots__ = ("sem", "val")

    def __init__(self, sem, val):
        self.sem = sem
        self.val = val


class Buf:
    __slots__ = ("name", "w", "r")

    def __init__(self, name):
        self.name = name
        self.w = None
        self.r = {}

    def add_read(self, ev):
        if ev is not None:
            self.r[ev.sem] = max(self.r.get(ev.sem, 0), ev.val)

    def readers(self):
        return [Ev(s, v) for s, v in self.r.items()]


class Sched:
    ENGS = ("pe", "act", "dve", "pool", "sp")

    def __init__(self, nc, es):
        self.nc = nc
        self.es = es
        self.ops = {e: [] for e in self.ENGS}
        self.sems = {}
        self.semcnt = {}
        self.waited = {e: {} for e in self.ENGS}
        for e in self.ENGS:
            self.newsem("E_" + e)

    def newsem(self, key):
        self.sems[key] = self.es.enter_context(self.nc.semaphore(key))
        self.semcnt[key] = 0
        return key

    def _waits(self, eng, deps):
        need = {}
        for d in deps:
            if d is None:
                continue
            need[d.sem] = max(need.get(d.sem, 0), d.val)
        w = []
        for s, v in need.items():
            if self.waited[eng].get(s, 0) < v:
                self.waited[eng][s] = v
                w.append((s, v))
        return w

    def _hazards(self, reads, writes, deps):
        al = list(deps)
        for b in reads:
            al.append(b.w)
        for b in writes:
            al.append(b.w)
            al.extend(b.readers())
        return al

    def _commit(self, ev, reads, writes):
        for b in reads:
            b.add_read(ev)
        for b in writes:
            b.w = ev
            b.r = {}

    def run(self, eng, fn, reads=(), writes=(), deps=()):
        w = self._waits(eng, self._hazards(reads, writes, deps))
        key = "E_" + eng
        self.semcnt[key] += 1
        ev = Ev(key, self.semcnt[key])
        self.ops[eng].append((w, fn, key, 1))
        self._commit(ev, reads, writes)
        return ev

    def dma(self, fn, semkey, reads=(), writes=(), deps=(), eng="sp"):
        w = self._waits(eng, self._hazards(reads, writes, deps))
        self.semcnt[semkey] += 16
        ev = Ev(semkey, self.semcnt[semkey])
        self.ops[eng].append((w, fn, semkey, 16))
        self._commit(ev, reads, writes)
        return ev

    def wait_only(self, eng, deps):
        w = self._waits(eng, deps)
        if w:
            self.ops[eng].append((w, None, None, 0))

    def emit(self):
        nc = self.nc
        sems = self.sems

        def run(engname):
            def body(eng):
                for (w, fn, inc, amt) in self.ops[engname]:
                    for (s, v) in w:
                        eng.wait_ge(sems[s], v)
                    if fn is None:
                        continue
                    inst = fn(eng)
                    inst.then_inc(sems[inc], amt)
            return body

        with nc.Block() as block:
            block.tensor(run("pe"))
            block.scalar(run("act"))
            block.vector(run("dve"))
            block.gpsimd(run("pool"))
            block.sync(run("sp"))


def build_program():
    nc = bass.Bass("TRN2", target_bir_lowering=False)

    def din(name, shape):
        return nc.dram_tensor(name, shape, F32, kind="ExternalInput").ap()

    def dout(name, shape):
        return nc.dram_tensor(name, shape, F32, kind="ExternalOutput").ap()

    x_p = din("x_p", [SEQ, D])
    x_s = din("x_s", [NSEQ_S, STEPS, D])
    st_conv = din("st_conv", [NSEQ_S, 3, D])
    st_lru = din("st_lru", [NSEQ_S, D])
    st_pool = din("st_pool", [NSEQ_S, 15, D])
    vecs = din("vecs", [16, D])
    fnorm = din("fnorm", [1, D])
    a_w_in = din("a_w_in", [D, 2 * D])
    a_w_r = din("a_w_r", [8, 128, 128])
    a_w_i = din("a_w_i", [8, 128, 128])
    a_w_out = din("a_w_out", [D, D])
    b_w_in = din("b_w_in", [D, 2 * D])
    b_w_grp = din("b_w_grp", [4, 256, 256])
    b_w_out = din("b_w_out", [D, D])

    y_p = dout("y_p", [SEQ, D])
    y_s = dout("y_s", [NSEQ_S, STEPS, D])
    o_conv_p = dout("o_conv_p", [3, D])
    o_lru_p = dout("o_lru_p", [1, D])
    o_pool_p = dout("o_pool_p", [15, D])
    o_conv_s = dout("o_conv_s", [NSEQ_S, 3, D])
    o_lru_s = dout("o_lru_s", [NSEQ_S, D])
    o_pool_s = dout("o_pool_s", [NSEQ_S, 15, D])

    with ExitStack() as es:
        S = Sched(nc, es)

        def sb(name, shape, dt):
            return es.enter_context(nc.sbuf_tensor(name, shape, dt))

        W_in = [sb("W_in0", [128, NCH, 2 * D], BF16), sb("W_in1", [128, NCH, 2 * D], BF16)]
        W_out = [sb("W_out0", [128, NCH, D], BF16), sb("W_out1", [128, NCH, D], BF16)]
        W_r = sb("W_r", [128, NCH, 128], BF16)
        W_i = sb("W_i", [128, NCH, 128], BF16)
        W_grp = sb("W_grp", [128, 8, 256], BF16)
        XBt = [sb("XB0", [128, 4, D], F32), sb("XB1", [128, 4, D], F32)]
        XSreg = sb("XSreg", [128, D], F32)
        UT = sb("UT", [128, NCH, T], BF16)
        HGt = [sb("HG0", [128, NCH, T], BF16), sb("HG1", [128, NCH, T], BF16)]
        HG = HGt[0]
        EXTW = 15 + T + 1
        EXTt = [sb("EXT%d" % i, [128, EXTW], F32) for i in range(2)]
        TMP = sb("TMP", [128, 4096], F32)
        TRt = [sb("TR%d" % i, [128, T], F32) for i in range(2)]
        XCBt = [sb("XCB%d" % i, [128, T], BF16) for i in range(2)]
        A2Mt = [sb("A2M%d" % i, [128, T], F32) for i in range(2)]
        Ht = [sb("H%d" % i, [128, T], F32) for i in range(2)]
        FN = sb("FN", [128, D], F32)
        VT = sb("VT", [128, NCH, 16], F32)
        DER = sb("DER", [128, 4, NCH], F32)
        DER2 = sb("DER2", [128, 2, NCH], F32)
        IDENT = sb("IDENT", [128, 128], F32)
        IDENTB = sb("IDENTB", [128, 128], BF16)
        IOTI = sb("IOTI", [128, 16], I32)
        IOTF = sb("IOTF", [128, 16], F32)
        RC = sb("RC", [128, 4, 16], F32)
        SSt = sb("SS", [128, 16], F32)
        TMP16 = sb("TMP16", [128, 16], F32)
        NEGH = sb("NEGH", [128, 4], F32)
        JUNK = sb("JUNK", [128, D], mybir.dt.uint8)
        H0p = sb("H0p", [128, NCH, 3], F32)
        H1p = sb("H1p", [128, NCH, 15], F32)
        HSTp = sb("HSTp", [128, NCH, 1], F32)
        H0s = sb("H0s", [128, NCH, 3 * NSEQ_S], F32)
        H1s = HGt[0][:].rearrange("p c t -> p (c t)").bitcast(F32)[:, 0:NCH * 15 * NSEQ_S].rearrange(
            "p (c r) -> p c r", r=15 * NSEQ_S)
        HSTs = sb("HSTs", [128, NCH, NSEQ_S], F32)

        PS = [es.enter_context(nc.psum_tensor("PS%d" % i, [128, 512], F32)) for i in range(8)]

        bW_in = [[Buf("win%d_%d" % (l, j)) for j in range(8)] for l in range(2)]
        bW_out = [[Buf("wout%d_%d" % (l, q)) for q in range(2)] for l in range(2)]
        bW_r, bW_i, bW_grp = Buf("wr"), Buf("wi"), Buf("wgrp")
        bXB = [[Buf("xb%d_%d" % (b, i)) for i in range(4)] for b in range(2)]
        bXS = [Buf("xs0"), Buf("xs1")]
        bUTs = [Buf("ut%d" % i) for i in range(4)]
        bHG2 = [[Buf("hg%d_%d" % (p_, c)) for c in range(NCH)] for p_ in range(2)]
        bHGc = bHG2[0]
        bJUNK = Buf("junk")
        bNEGH = Buf("negh")
        N2_ENG = "pool"
        N4_ENG = ("act", "dve")
        bEXT = [Buf("ext0"), Buf("ext1")]
        bSG = [Buf("sg%d" % i) for i in range(4)]
        bV = [Buf("v%d" % i) for i in range(2)]
        bXC = [Buf("xc%d" % i) for i in range(2)]
        bTR = [Buf("tr%d" % i) for i in range(2)]
        bXCB = [Buf("xcb%d" % i) for i in range(2)]
        bA2M = [Buf("a2m%d" % i) for i in range(2)]
        bH = [Buf("h%d" % i) for i in range(2)]
        bPS = [Buf("ps%d" % i) for i in range(8)]
        bFN, bVT, bDER, bID, bIDB, bRC, bT16 = (Buf("fn"), Buf("vt"), Buf("der"), Buf("id"), Buf("idb"),
                                                 Buf("rc"), Buf("t16"))
        bSSc = [Buf("ss%d" % i) for i in range(16)]
        bH0 = {True: [Buf("h0p%d" % c) for c in range(NCH)], False: [Buf("h0s%d" % c) for c in range(NCH)]}
        bH1 = {True: [Buf("h1p%d" % c) for c in range(NCH)], False: [Buf("h1s%d" % c) for c in range(NCH)]}
        bHST = {True: [Buf("hstp%d" % c) for c in range(NCH)], False: [Buf("hsts%d" % c) for c in range(NCH)]}

        XS = [XSreg[:, 0:512].bitcast(BF16), XSreg[:, 512:1024].bitcast(BF16)]
        SG = [TMP[:, i * 512:(i + 1) * 512] for i in range(4)]
        V = [TMP[:, 2048 + i * 512:2048 + (i + 1) * 512] for i in range(2)]
        XC = [TMP[:, 3072 + i * 512:3072 + (i + 1) * 512] for i in range(2)]
        SR = [TMP[:, i * 1024:(i + 1) * 1024] for i in range(4)]
        bSR = [[bSG[0], bSG[1]], [bSG[2], bSG[3]], [bV[0], bV[1]], [bXC[0], bXC[1]]]
        PSB = [p[:].bitcast(BF16) for p in PS]

        for k in ("ld_vecs", "ld_fn", "ld_x0", "ld_x1", "st_x0", "st_x1", "wst0", "wst1", "ld_sr", "st_m0", "st_m1", "st_m2", "st_m3"):
            S.newsem(k)

        for s_ in range(4):
            k_ = S.newsem("ld_x0s%d" % s_)
            S.dma(lambda e, s_=s_: e.dma_start(out=XBt[0][:, s_, :], in_=x_p[s_ * 128:(s_ + 1) * 128, :]), k_,
                  writes=[bXB[0][s_]])
        VROWS = XSreg[0:16, :]
        S.dma(lambda e: e.dma_start(out=VROWS, in_=vecs), "ld_vecs", writes=[bXS[0], bXS[1]])
        S.dma(lambda e: e.dma_start(out=FN[:], in_=fnorm.partition_broadcast(128)), "ld_fn", writes=[bFN])

        S.run("pool", lambda e: e.memset(NEGH[:], -0.5), writes=[bNEGH])
        S.run("pool", lambda e: e.memset(IDENT[:], 0.0), writes=[bID])
        S.run("pool", lambda e: e.affine_select(out=IDENT[:], in_=IDENT[:], pattern=[[-1, 128]],
                                                compare_op=ALU.not_equal, fill=1.0, base=0,
                                                channel_multiplier=1), reads=[bID], writes=[bID])
        S.run("dve", lambda e: e.tensor_copy(out=IDENTB[:], in_=IDENT[:]), reads=[bID], writes=[bIDB])
        for t_, b_ in ((H0p, bH0[True]), (H1p, bH1[True]), (HSTp, bHST[True])):
            S.run("pool", lambda e, t_=t_: e.memset(t_[:], 0.0), writes=b_)

        def tr_vecs(e):
            last = None
            for c in range(NCH):
                last = e.transpose(out=PS[6][:, c * 16:(c + 1) * 16], in_=VROWS[:, c * 128:(c + 1) * 128],
                                   identity=IDENT[0:16, 0:16])
            return last
        S.run("pe", tr_vecs, reads=[bXS[0], bXS[1], bID], writes=[bPS[6]])
        S.run("dve", lambda e: e.tensor_copy(out=VT[:].rearrange("p c r -> p (c r)"), in_=PS[6][:, 0:128]),
              reads=[bPS[6]], writes=[bVT])

        def vcol(c, r):
            return VT[:, c, r:r + 1]

        lamv = VT[:, :, V_LAM]
        K1, HBR, HBI, SCR = DER[:, 0, :], DER[:, 1, :], DER[:, 2, :], DER[:, 3, :]
        BGS = DER2[:, 1, :]
        K2 = DER2[:, 0, :]

        def late_prologue():
            S.run("act", lambda e: e.activation(out=SCR, in_=lamv, func=AF.Abs), reads=[bVT], writes=[bDER])
            S.run("act", lambda e: e.activation(out=SCR, in_=SCR, func=AF.Exp, scale=-1.0), reads=[bDER], writes=[bDER])
            S.run("act", lambda e: e.activation(out=SCR, in_=SCR, func=AF.Ln, bias=1.0), reads=[bDER], writes=[bDER])
            S.run("dve", lambda e: e.tensor_scalar(out=K1, in0=lamv, scalar1=-1.0, scalar2=0.0, op0=ALU.mult,
                                                   op1=ALU.max), reads=[bVT, bDER], writes=[bDER])
            S.run("dve", lambda e: e.tensor_tensor(out=K1, in0=K1, in1=SCR, op=ALU.add), reads=[bDER], writes=[bDER])
            S.run("dve", lambda e: e.tensor_scalar(out=K1, in0=K1, scalar1=-4.0, scalar2=None, op0=ALU.mult),
                  reads=[bDER], writes=[bDER])
            S.run("dve", lambda e: e.tensor_tensor(out=BGS, in0=VT[:, :, V_BGRP], in1=VT[:, :, V_BSCALE], op=ALU.mult),
                  reads=[bVT], writes=[bDER])
            S.run("dve", lambda e: e.tensor_scalar(out=K2, in0=K1, scalar1=2.0, scalar2=None, op0=ALU.mult),
                  reads=[bDER], writes=[bDER])
            S.run("dve", lambda e: e.tensor_scalar(out=HBR, in0=VT[:, :, V_BR], scalar1=0.5, scalar2=None,
                                                   op0=ALU.mult), reads=[bVT, bDER], writes=[bDER])
            S.run("dve", lambda e: e.tensor_scalar(out=HBI, in0=VT[:, :, V_BI], scalar1=0.5, scalar2=None,
                                                   op0=ALU.mult), reads=[bVT, bDER], writes=[bDER])
            S.run("pool", lambda e: e.iota(IOTI[:], pattern=[[1, 16]], base=1, channel_multiplier=0), writes=[bRC])
            S.run("pool", lambda e: e.tensor_copy(out=IOTF[:], in_=IOTI[:]), reads=[bRC], writes=[bRC])
            for g, Wd in enumerate(POOL_W):
                S.run("pool", lambda e, g=g, Wd=Wd: e.memset(RC[:, g, :], float(Wd)), writes=[bRC])

        rc_done = set()

        def rc_group(g):
            if rc_done:
                return
            rc_done.add(g)
            iof = IOTF[:].rearrange("p (o j) -> p o j", o=1).to_broadcast([128, len(POOL_W), 16])
            S.run("dve", lambda e: e.tensor_tensor(out=RC[:, :, :], in0=RC[:, :, :], in1=iof, op=ALU.min), reads=[bRC], writes=[bRC])
            S.run("dve", lambda e: e.reciprocal(out=RC[:, :, :], in_=RC[:, :, :]), reads=[bRC], writes=[bRC])


        def rows_to_fm(src, bsrc, nrows, dst_fn, bdst, psi, multi=False, eng="dve"):
            for half in range(2):
                def trf(e, half=half):
                    last = None
                    for cc in range(4):
                        c = half * 4 + cc
                        last = e.transpose(out=PS[psi][:, cc * 128:cc * 128 + nrows],
                                           in_=src[0:nrows, c * 128:(c + 1) * 128],
                                           identity=IDENT[0:nrows, 0:nrows])
                    return last
                S.run("pe", trf, reads=list(bsrc) + [bID], writes=[bPS[psi]])
                for cc in range(4):
                    c = half * 4 + cc
                    if eng == "dve":
                        S.run("dve", lambda e, c=c, cc=cc: e.tensor_copy(out=dst_fn(c), in_=PS[psi][:, cc * 128:cc * 128 + nrows]),
                              reads=[bPS[psi]], writes=(bdst[c] if multi else [bdst[c]]))
                    else:
                        S.run("act", lambda e, c=c, cc=cc: e.activation(out=dst_fn(c), in_=PS[psi][:, cc * 128:cc * 128 + nrows],
                                                                        func=AF.Copy),
                              reads=[bPS[psi]], writes=(bdst[c] if multi else [bdst[c]]))

        wq = []

        def wdma(src, dst, bdst, defer):
            k = S.newsem("w%d" % len(S.sems))
            bl = list(bdst) if isinstance(bdst, (list, tuple)) else [bdst]

            def go():
                S.dma(lambda e: e.dma_start(out=dst, in_=src), k, writes=bl, eng="pool")
            if defer:
                wq.append(go)
            else:
                go()

        def w_in_view(w, half, blk):
            c0 = half * D + blk * 512
            return w.rearrange("(kc p) n -> p kc n", p=128)[:, :, c0:c0 + 512], c0

        def load_layer_weights(l, w_in_d, w_out_d, defer):
            for blk in range(2):
                if l == 0 and blk == 0:
                    wv_ = w_in_d.rearrange("(kc p) n -> p kc n", p=128)
                    for pr_ in range(2):
                        for half in range(2):
                            c0 = half * D + pr_ * 256
                            wdma(wv_[:, :, c0:c0 + 256], W_in[l][:, :, c0:c0 + 256], bW_in[l][pr_ * 2 + half], defer)
                else:
                    for half in range(2):
                        src, c0 = w_in_view(w_in_d, half, blk)
                        wdma(src, W_in[l][:, :, c0:c0 + 512],
                             [bW_in[l][(2 * blk) * 2 + half], bW_in[l][(2 * blk + 1) * 2 + half]], defer)
                if blk == 0:
                    if l == 0:
                        wdma(a_w_r.rearrange("n i j -> i n j"), W_r[:], bW_r, defer)
                        wdma(a_w_i.rearrange("n i j -> i n j"), W_i[:], bW_i, defer)
                    else:
                        wdma(b_w_grp.rearrange("g (kc p) j -> p (g kc) j", p=128), W_grp[:], bW_grp, defer)
            wv = w_out_d.rearrange("(kc p) n -> p kc n", p=128)
            for q in range(2):
                wdma(wv[:, 4 * q:4 * q + 4, :], W_out[l][:, 4 * q:4 * q + 4, :], bW_out[l][q], defer)

        load_layer_weights(0, a_w_in, a_w_out, False)
        load_layer_weights(1, b_w_in, b_w_out, True)

        def maybe_casts():
            if wq:
                wq.pop(0)()

        class Tile:
            pass

        tiles = []
        for i in range(4):
            t = Tile()
            t.prompt, t.first, t.T, t.S, t.nsub, t.npart, t.t0, t.xb = True, i == 0, T, 1, 4, 128, i * T, i % 2
            tiles.append(t)
        t = Tile()
        t.prompt, t.first, t.T, t.S, t.nsub, t.npart, t.t0, t.xb = False, False, TS, NSEQ_S, 1, TS, 0, 0
        tiles.append(t)

        for b_ in range(2):
            for s_ in range(4):
                S.newsem("ldx%d_%d" % (b_, s_))
                S.newsem("stx%d_%d" % (b_, s_))
            S.newsem("stx%d_0a" % b_)

        def load_x_sub(tl, s):
            b = tl.xb
            if tl.prompt:
                src = x_p[tl.t0 + s * 128:tl.t0 + (s + 1) * 128, :]
                S.dma(lambda e: e.dma_start(out=XBt[b][:, s, :], in_=src), "ldx%d_%d" % (b, s), writes=[bXB[b][s]])
            elif s == 0:
                for st in range(STEPS):
                    S.dma(lambda e, st=st: e.dma_start(out=XBt[b][st * 16:(st + 1) * 16, 0, :], in_=x_s[:, st, :]),
                          "ldx%d_0" % b, writes=[bXB[b][0]])

        def load_x(tl, extra=()):
            for s in range(tl.nsub):
                load_x_sub(tl, s)

        def store_y_sub(tl, s):
            b = tl.xb
            if tl.prompt:
                dst = y_p[tl.t0 + s * 128:tl.t0 + (s + 1) * 128, :]
                S.dma(lambda e: e.dma_start(out=dst, in_=XBt[b][:, s, :]), "stx%d_%d" % (b, s), reads=[bXB[b][s]])
            else:
                for st in range(STEPS):
                    S.dma(lambda e, st=st: e.dma_start(out=y_s[:, st, :], in_=XBt[b][st * 16:(st + 1) * 16, 0, :]),
                          "stx%d_0%s" % (b, "a" if st % 2 else ""), reads=[bXB[b][0]], eng=("act" if st % 2 else "sp"))

        sscol = {"i": 0}
        rstd_of = {}

        def N1(tl, s, key):
            b, np_ = tl.xb, tl.npart
            col = sscol["i"] % 16
            sscol["i"] += 1
            X = XBt[b][0:np_, s, :]
            ssc = SSt[0:np_, col:col + 1]
            bSS = bSSc[col]
            junk = JUNK[0:np_, :]
            S.run("act", lambda e: e.activation(out=junk, in_=X, func=AF.Square, accum_out=ssc),
                  reads=[bXB[b][s]], writes=[bJUNK, bSS])
            S.run("act", lambda e: e.activation(out=ssc, in_=ssc, func=AF.Ln, scale=1.0 / D, bias=EPS),
                  reads=[bSS], writes=[bSS])
            S.run("act", lambda e: e.activation(out=ssc, in_=ssc, func=AF.Exp, scale=-0.5), reads=[bSS], writes=[bSS])
            rstd_of[key] = (ssc, bSS)

        def N1_batch(tl, lkey, subs=None):
            b, np_ = tl.xb, tl.npart
            subs = list(range(tl.nsub)) if subs is None else list(subs)
            ns = len(subs)
            g0 = (sscol["i"] + 3) // 4 * 4 % 16
            sscol["i"] = g0 + 4
            grp = SSt[0:np_, g0:g0 + ns]
            bgrp = [bSSc[g0 + i_] for i_ in range(ns)]
            for i_, s_ in enumerate(subs):
                X = XBt[b][0:np_, s_, :]
                ssc = SSt[0:np_, g0 + i_:g0 + i_ + 1]
                junk = JUNK[0:np_, :]
                S.run("act", lambda e, X=X, ssc=ssc, junk=junk: e.activation(out=junk, in_=X, func=AF.Square, accum_out=ssc),
                      reads=[bXB[b][s_]], writes=[bJUNK, bgrp[i_]])
            S.run("pool", lambda e: e.tensor_scalar(out=grp, in0=grp, scalar1=1.0 / D, scalar2=EPS, op0=ALU.mult, op1=ALU.add),
                  reads=bgrp, writes=bgrp)
            S.run("pool", lambda e: e.tensor_tensor(out=grp, in0=grp, in1=NEGH[0:np_, 0:ns], op=ALU.pow),
                  reads=bgrp + [bNEGH], writes=bgrp)
            for i_, s_ in enumerate(subs):
                rstd_of[(lkey, s_)] = (SSt[0:np_, g0 + i_:g0 + i_ + 1], bgrp[i_])

        n2_override = [None]

        def N2(tl, s, key):
            b, np_ = tl.xb, tl.npart
            ssc, bSS = rstd_of[key]
            X = XBt[b][0:np_, s, :]
            xs = XS[s % 2][0:np_, :]
            S.run(n2_override[0] or N2_ENG, lambda e: e.tensor_scalar(out=xs, in0=X, scalar1=ssc, scalar2=0.0, op0=ALU.mult, op1=ALU.add),
                  reads=[bXB[b][s], bSS], writes=[bXS[s % 2]])

        def N3(tl, s):
            np_ = tl.npart
            xs = XS[s % 2][0:np_, :]
            tb = s % 2
            tbv = PSB[tb][:, 0:NCH * np_].rearrange("p (k t) -> p k t", t=np_)

            def trf(e):
                last = None
                for k in range(NCH):
                    last = e.transpose(out=tbv[:, k, :], in_=xs[:, k * 128:(k + 1) * 128],
                                       identity=IDENTB[0:np_, 0:np_])
                return last
            S.run("pe", trf, reads=[bXS[s % 2], bIDB], writes=[bPS[tb]])

        def N4(tl, s, grow):
            np_ = tl.npart
            tb = s % 2
            tbv = PSB[tb][:, 0:NCH * np_].rearrange("p (k t) -> p k t", t=np_)
            gbc = VT[:, :, grow:grow + 1].to_broadcast([128, NCH, np_])
            S.run("dve", lambda e: e.tensor_tensor(out=UT[:, :, s * 128:s * 128 + np_], in0=tbv, in1=gbc, op=ALU.mult),
                  reads=[bPS[tb], bVT], writes=[bUTs[s]])

        def norm_pipeline(tl, lkey, n1_done=(), n2_done=()):
            ns = tl.nsub
            grow = V_ANORM if lkey == "l0" else V_BNORM
            for s in range(ns):
                if s not in n1_done:
                    N1(tl, s, (lkey, s))
                if s not in n2_done:
                    N2(tl, s, (lkey, s))
                N3(tl, s)
                if s >= 1:
                    N4(tl, s - 1, grow)
            N4(tl, ns - 1, grow)

        def inproj(l, tl, m, psi):
            Tt = tl.T

            def f(e):
                last = None
                for k in range(NCH):
                    last = e.matmul(PS[psi][:, 0:Tt], lhsT=W_in[l][:, k, m * 128:(m + 1) * 128], rhs=UT[:, k, 0:Tt],
                                    start=(k == 0), stop=(k == NCH - 1))
                return last
            widx = (m // 2) * 2 if m < NCH else ((m - NCH) // 2) * 2 + 1
            S.run("pe", f, reads=bUTs + [bW_in[l][widx]], writes=[bPS[psi]])

        NACC = {0: 6, 1: 6}

        PORD = {0: [0, 1, 2, 3], 1: [3, 2, 1, 0]}
        KORD = {l_: [c for j_ in PORD[l_] for c in (2 * j_, 2 * j_ + 1)] for l_ in (0, 1)}

        def out_part1(l, tl):
            np_ = tl.npart
            groups = [(s, half) for s in range(tl.nsub) for half in range(2)]
            groups = [g_ for g_ in groups if 2 * g_[0] + g_[1] < NACC[l]]

            def f(e):
                last = None
                for (s, half) in groups:
                    psi = 2 * s + half
                    for i_, k in enumerate(KORD[l][0:6]):
                        last = e.matmul(PS[psi][0:np_, :], lhsT=HGt[l][:, k, s * 128:s * 128 + np_],
                                        rhs=W_out[l][:, k, half * 512:(half + 1) * 512],
                                        start=(i_ == 0), stop=False)
                return last
            S.run("pe", f, reads=[bHG2[l][k] for k in KORD[l][0:6]] + bW_out[l],
                  writes=[bPS[2 * s + half] for (s, half) in groups])

        def out_part2(l, tl, s):
            b, np_ = tl.xb, tl.npart
            hs = l if tl.prompt else 1
            for half in range(2):
                psi = 2 * s + half
                k0 = 6 if (psi < NACC[l] and tl.opened) else 0

                def f(e, half=half, psi=psi, k0=k0):
                    last = None
                    for i_ in range(k0, NCH):
                        k = KORD[l][i_]
                        last = e.matmul(PS[psi][0:np_, :], lhsT=HGt[hs][:, k, s * 128:s * 128 + np_],
                                        rhs=W_out[l][:, k, half * 512:(half + 1) * 512],
                                        start=(i_ == 0), stop=(i_ == NCH - 1))
                    return last
                S.run("pe", f, reads=bHG2[hs] + bW_out[l], writes=[bPS[psi]])
                Xh = XBt[b][0:np_, s, half * 512:(half + 1) * 512]
                S.run("dve", lambda e, Xh=Xh, psi=psi: e.tensor_tensor(out=Xh, in0=PS[psi][0:np_, :], in1=Xh, op=ALU.add),
                      reads=[bPS[psi], bXB[b][s]], writes=[bXB[b][s]])

        def final_piece_a(tl, s):
            b, np_ = tl.xb, tl.npart
            col = sscol["i"] % 16
            sscol["i"] += 1
            X = XBt[b][0:np_, s, :]
            ssc = SSt[0:np_, col:col + 1]
            bSS = bSSc[col]
            junk = JUNK[0:np_, :]
            S.run("act", lambda e: e.activation(out=junk, in_=X, func=AF.Square, accum_out=ssc),
                  reads=[bXB[b][s]], writes=[bJUNK, bSS])
            S.run("pool", lambda e: e.tensor_scalar(out=ssc, in0=ssc, scalar1=1.0 / D, scalar2=EPS, op0=ALU.mult, op1=ALU.add),
                  reads=[bSS], writes=[bSS])
            S.run("pool", lambda e: e.tensor_tensor(out=ssc, in0=ssc, in1=NEGH[0:np_, 0:1], op=ALU.pow),
                  reads=[bSS, bNEGH], writes=[bSS])
            rstd_of[("fin", id(tl), s)] = (ssc, bSS)

        def final_piece_b(tl, s, next_tl):
            b, np_ = tl.xb, tl.npart
            ssc, bSS = rstd_of[("fin", id(tl), s)]
            X = XBt[b][0:np_, s, :]
            S.run("dve", lambda e: e.scalar_tensor_tensor(out=X, in0=X, scalar=ssc, in1=FN[0:np_, :], op0=ALU.mult,
                                                          op1=ALU.mult), reads=[bXB[b][s], bSS, bFN], writes=[bXB[b][s]])
            store_y_sub(tl, s)
            if next_tl is not None:
                load_x_sub(next_tl, s)

        def final_piece(tl, s, next_tl):
            final_piece_a(tl, s)
            final_piece_b(tl, s, next_tl)

        def A1pe_xb(l, tl, j):
            for ci, c in enumerate((2 * j, 2 * j + 1)):
                inproj(l, tl, c, 2 + 2 * ci)

        def A1pe_g(l, tl, j):
            for ci, c in enumerate((2 * j, 2 * j + 1)):
                inproj(l, tl, NCH + c, 3 + 2 * ci)

        def A1pe(l, tl, j):
            A1pe_xb(l, tl, j)
            A1pe_g(l, tl, j)

        def A1evac(l, tl, j):
            Tt, St, pr = tl.T, tl.S, tl.prompt
            HL = (3 if l == 0 else 15) * St
            Hh = (H0p if pr else H0s) if l == 0 else (H1p if pr else H1s)
            bHh = bH0[pr] if l == 0 else bH1[pr]
            for ci, c in enumerate((2 * j, 2 * j + 1)):
                ext = EXTt[ci]
                S.run("pool", lambda e, ext=ext, c=c: e.tensor_copy(out=ext[:, 0:HL], in_=Hh[:, c, :]),
                      reads=[bHh[c]], writes=[bEXT[ci]])
                if l == 0:
                    S.run("dve", lambda e, ext=ext, ci=ci: e.tensor_copy(out=ext[:, HL:HL + Tt], in_=PS[2 + 2 * ci][:, 0:Tt]),
                          reads=[bPS[2 + 2 * ci]], writes=[bEXT[ci]])
                else:
                    S.run("act", lambda e, ext=ext, ci=ci: e.activation(out=ext[:, HL:HL + Tt], in_=PS[2 + 2 * ci][:, 0:Tt],
                                                                         func=AF.Copy),
                          reads=[bPS[2 + 2 * ci]], writes=[bEXT[ci]])
                S.run("pool", lambda e, ext=ext, c=c: e.tensor_copy(out=Hh[:, c, :], in_=ext[:, Tt:Tt + HL]),
                      reads=[bEXT[ci]], writes=[bHh[c]])
                if l == 0:
                    conv_tap(tl, ci, c, 0)
            if l == 0:
                for k in range(1, 4):
                    for ci, c in enumerate((2 * j, 2 * j + 1)):
                        conv_tap(tl, ci, c, k)

        def A1silu(l, tl, j):
            Tt = tl.T
            for ci, c in enumerate((2 * j, 2 * j + 1)):
                q = (2 * j + ci) % 4
                S.run("act", lambda e, q=q, ci=ci: e.activation(out=SG[q][:, 0:Tt], in_=PS[3 + 2 * ci][:, 0:Tt],
                                                                 func=AF.Silu),
                      reads=[bPS[3 + 2 * ci]], writes=[bSG[q]])

        def conv_tap(tl, ci, c, k):
            Tt, St = tl.T, tl.S
            ext = EXTt[ci]
            xc = XC[ci][:, 0:Tt]
            if k == 0:
                S.run("dve", lambda e: e.tensor_scalar(
                    out=xc, in0=ext[:, 0:Tt], scalar1=vcol(c, V_CW0), scalar2=vcol(c, V_CB),
                    op0=ALU.mult, op1=ALU.add), reads=[bEXT[ci], bVT], writes=[bXC[ci]])
            else:
                S.run("dve", lambda e: e.scalar_tensor_tensor(
                    out=xc, in0=ext[:, k * St:k * St + Tt], scalar=vcol(c, V_CW0 + k), in1=xc,
                    op0=ALU.mult, op1=ALU.add), reads=[bEXT[ci], bVT, bXC[ci]], writes=[bXC[ci]])

        def conv(tl, ci, c):
            Tt, St = tl.T, tl.S
            ext = EXTt[ci]
            xc = XC[ci][:, 0:Tt]
            S.run("dve", lambda e: e.tensor_scalar(
                out=xc, in0=ext[:, 0:Tt], scalar1=vcol(c, V_CW0), scalar2=vcol(c, V_CB),
                op0=ALU.mult, op1=ALU.add), reads=[bEXT[ci], bVT], writes=[bXC[ci]])
            for k in range(1, 4):
                S.run("dve", lambda e, k=k: e.scalar_tensor_tensor(
                    out=xc, in0=ext[:, k * St:k * St + Tt], scalar=vcol(c, V_CW0 + k), in1=xc,
                    op0=ALU.mult, op1=ALU.add), reads=[bEXT[ci], bVT, bXC[ci]], writes=[bXC[ci]])

        def CASTRI(tl, j):
            Tt = tl.T
            for ci, c in enumerate((2 * j, 2 * j + 1)):
                xc = XC[ci][:, 0:Tt]
                S.run("act", lambda e, xc=xc, ci=ci: e.activation(out=XCBt[ci][:, 0:Tt], in_=xc, func=AF.Copy),
                      reads=[bXC[ci]], writes=[bXCB[ci]])
            maybe_casts()

        def TANHV(tl, j):
            Tt = tl.T
            for ci, c in enumerate((2 * j, 2 * j + 1)):
                xc = XC[ci][:, 0:Tt]

                def fri(e, c=c, ci=ci):
                    e.matmul(PS[6][:, 0:Tt], lhsT=W_r[:, c, :], rhs=XCBt[ci][:, 0:Tt], start=True, stop=True)
                    return e.matmul(PS[7][:, 0:Tt], lhsT=W_i[:, c, :], rhs=XCBt[ci][:, 0:Tt], start=True, stop=True)
                S.run("pe", fri, reads=[bXCB[ci], bW_r, bW_i], writes=[bPS[6], bPS[7]])
                S.run("act", lambda e, c=c, ci=ci: e.activation(out=TRt[ci][:, 0:Tt], in_=PS[6][:, 0:Tt], func=AF.Tanh,
                                                                 scale=0.5, bias=HBR[:, c:c + 1]),
                      reads=[bPS[6], bDER], writes=[bTR[ci]])
                S.run("act", lambda e, c=c, ci=ci: e.activation(out=V[ci][:, 0:Tt], in_=PS[7][:, 0:Tt], func=AF.Tanh,
                                                                 scale=0.5, bias=HBI[:, c:c + 1]),
                      reads=[bPS[7], bDER], writes=[bV[ci]])
                S.run("dve", lambda e, ci=ci, xc=xc: e.scalar_tensor_tensor(
                    out=V[ci][:, 0:Tt], in0=V[ci][:, 0:Tt], scalar=1.0, in1=xc, op0=ALU.add, op1=ALU.mult),
                    reads=[bV[ci], bXC[ci]], writes=[bV[ci]])

        def E0a(tl, j):
            Tt = tl.T
            chunks = (2 * j, 2 * j + 1)
            for ci, c in enumerate(chunks):
                tr = TRt[ci][:, 0:Tt]
                m = A2Mt[ci][:, 0:Tt]
                S.run("act", lambda e, tr=tr, m=m, c=c: e.activation(out=m, in_=tr, func=AF.Exp, scale=K2[:, c:c + 1],
                                                                      bias=K2[:, c:c + 1]),
                      reads=[bTR[ci], bDER], writes=[bA2M[ci]])
                S.run("act", lambda e, m=m: e.activation(out=m, in_=m, func=AF.Ln, scale=-1.0, bias=1.0),
                      reads=[bA2M[ci]], writes=[bA2M[ci]])
                S.run("act", lambda e, m=m: e.activation(out=m, in_=m, func=AF.Exp, scale=0.5, bias=LN_HALF),
                      reads=[bA2M[ci]], writes=[bA2M[ci]])
                if tl.first:
                    S.run("pool", lambda e, ci=ci: e.memset(A2Mt[ci][:, 0:1], 0.5), reads=[bA2M[ci]], writes=[bA2M[ci]])
                S.run("act", lambda e, tr=tr, c=c: e.activation(out=tr, in_=tr, func=AF.Exp, scale=K1[:, c:c + 1],
                                                                 bias=K1[:, c:c + 1]),
                      reads=[bTR[ci], bDER], writes=[bTR[ci]])

        def E0b(tl, j):
            Tt, St, pr = tl.T, tl.S, tl.prompt
            HST = HSTp if pr else HSTs
            for ci, c in enumerate((2 * j, 2 * j + 1)):
                q = (2 * j + ci) % 4
                a = TRt[ci][:, 0:Tt]
                m = A2Mt[ci][:, 0:Tt]
                bt = V[ci][:, 0:Tt]
                h = Ht[ci][:, 0:Tt]
                S.run("dve", lambda e, bt=bt, m=m: e.tensor_tensor(out=bt, in0=bt, in1=m, op=ALU.mult),
                      reads=[bV[ci], bA2M[ci]], writes=[bV[ci]])
                if pr:
                    S.run("dve", lambda e, h=h, a=a, bt=bt, c=c: e.tensor_tensor_scan(
                        out=h, data0=a, data1=bt, initial=HST[:, c, 0:1], op0=ALU.mult, op1=ALU.add),
                        reads=[bTR[ci], bV[ci], bHST[pr][c]], writes=[bH[ci]])
                else:
                    for st in range(STEPS):
                        sl = slice(st * St, (st + 1) * St)
                        prev = HST[:, c, :] if st == 0 else Ht[ci][:, (st - 1) * St:st * St]
                        S.run("dve", lambda e, ci=ci, sl=sl, prev=prev: e.tensor_tensor(
                            out=Ht[ci][:, sl], in0=TRt[ci][:, sl], in1=prev, op=ALU.mult),
                            reads=[bTR[ci], bHST[pr][c], bH[ci]], writes=[bH[ci]])
                        S.run("dve", lambda e, ci=ci, sl=sl: e.tensor_tensor(
                            out=Ht[ci][:, sl], in0=Ht[ci][:, sl], in1=V[ci][:, sl], op=ALU.add),
                            reads=[bV[ci], bH[ci]], writes=[bH[ci]])
                S.run("pool", lambda e, h=h, c=c: e.tensor_copy(out=HST[:, c, :], in_=h[:, Tt - St:Tt]),
                      reads=[bH[ci]], writes=[bHST[pr][c]])
                S.run("dve", lambda e, h=h, c=c, q=q: e.tensor_tensor(
                    out=HG[:, c, 0:Tt], in0=h, in1=SG[q][:, 0:Tt], op=ALU.mult),
                    reads=[bH[ci], bSG[q]], writes=[bHGc[c]])
                maybe_casts()

        PTW = [TMP[:, 2048 + i * EXTW:2048 + (i + 1) * EXTW] for i in range(2)]
        bPTW = [Buf("ptw%d" % i) for i in range(2)]

        TRB = [t_[:].bitcast(BF16) for t_ in TRt]

        def pooled_buf(j, ci, Tt):
            if PORD[1].index(j) % 2 == 0:
                return XCBt[ci][:, 0:Tt], bXCB[ci]
            return TRB[ci][:, 0:Tt], bTR[ci]

        def P1(tl, j, only=None):
            Tt, St = tl.T, tl.S
            HL = 15 * St
            Wd = POOL_W[j]
            for ci, c in enumerate((2 * j, 2 * j + 1)):
                if only is not None and ci != only:
                    continue
                ext = EXTt[ci]
                pa, pb = PTW[0], PTW[1]
                bpa, bpb = bPTW[0], bPTW[1]
                src, bsrc = ext, bEXT[ci]
                lvl = 1
                dst, bdst = pa, bpa
                while lvl < Wd:
                    lo = HL - (Wd - 2 * lvl) * St
                    hi = HL + Tt
                    sh = lvl * St
                    S.run("dve", lambda e, dst=dst, src=src, lo=lo, hi=hi, sh=sh: e.tensor_tensor(
                        out=dst[:, lo:hi], in0=src[:, lo:hi], in1=src[:, lo - sh:hi - sh], op=ALU.add),
                        reads=[bsrc], writes=[bdst])
                    src, bsrc = dst, bdst
                    dst, bdst = (pb, bpb) if dst is pa else (pa, bpa)
                    lvl *= 2
                pooled, bpooled = pooled_buf(j, ci, Tt)
                S.run("dve", lambda e, src=src, ext=ext, pooled=pooled, Wd=Wd: e.scalar_tensor_tensor(
                    out=pooled, in0=src[:, HL:HL + Tt], scalar=1.0 / Wd, in1=ext[:, HL:HL + Tt],
                    op0=ALU.mult, op1=ALU.subtract), reads=[bsrc, bEXT[ci]], writes=[bpooled])
                if tl.first:
                    rc_group(j)
                    tmp = TMP16[:, 0:16]
                    S.run("dve", lambda e, src=src, tmp=tmp, j=j: e.tensor_tensor(out=tmp, in0=src[:, HL:HL + 16],
                                                                                   in1=RC[:, j, :], op=ALU.mult),
                          reads=[bsrc, bRC], writes=[bT16])
                    S.run("dve", lambda e, ext=ext, tmp=tmp, pooled=pooled: e.tensor_tensor(
                        out=pooled[:, 0:16], in0=tmp, in1=ext[:, HL:HL + 16], op=ALU.subtract),
                        reads=[bT16, bEXT[ci]], writes=[bpooled])

        def Z1pe(tl, j):
            Tt = tl.T
            pl = [pooled_buf(j, kc, Tt) for kc in range(2)]
            for mi in range(2):
                def fz(e, mi=mi, j=j):
                    last = None
                    for kc in range(2):
                        last = e.matmul(PS[6 + mi][:, 0:Tt], lhsT=W_grp[:, 2 * j + kc, mi * 128:(mi + 1) * 128],
                                        rhs=pl[kc][0], start=(kc == 0), stop=(kc == 1))
                    return last
                S.run("pe", fz, reads=[pl[0][1], pl[1][1], bW_grp], writes=[bPS[6 + mi]])

        def Z1act(tl, j):
            Tt = tl.T
            for mi, c in enumerate((2 * j, 2 * j + 1)):
                zt = A2Mt[mi][:, 0:Tt]
                S.run("act", lambda e, mi=mi, c=c, zt=zt: e.activation(out=zt, in_=PS[6 + mi][:, 0:Tt], func=AF.Identity,
                                                                        scale=vcol(c, V_BSCALE), bias=BGS[:, c:c + 1]),
                      reads=[bPS[6 + mi], bVT, bDER], writes=[bA2M[mi]])

        def Z1dve(tl, j):
            Tt = tl.T
            for mi, c in enumerate((2 * j, 2 * j + 1)):
                q = (2 * j + mi) % 4
                zt = A2Mt[mi][:, 0:Tt]
                S.run("dve", lambda e, c=c, q=q, zt=zt: e.tensor_tensor(out=HGt[1][:, c, 0:Tt], in0=zt, in1=SG[q][:, 0:Tt],
                                                                        op=ALU.mult),
                      reads=[bA2M[mi], bSG[q]], writes=[bHG2[1][c]])

        alias_bufs = [bV[0], bV[1], bXC[0], bXC[1]]

        pending_final = []

        def flush_final(n=99):
            while pending_final and n > 0:
                final_piece(*pending_final.pop(0))
                n -= 1

        drainq = []

        def drain1(n=1):
            while drainq and n > 0:
                t_, s_, half = drainq.pop(0)
                psi = half
                np_ = t_.npart

                def f(e, t_=t_, s_=s_, half=half, psi=psi, np_=np_):
                    last = None
                    for k in range(NCH):
                        last = e.matmul(PS[psi][0:np_, :], lhsT=HGt[1][:, k, s_ * 128:s_ * 128 + np_],
                                        rhs=W_out[1][:, k, half * 512:(half + 1) * 512], start=(k == 0), stop=(k == NCH - 1))
                    return last
                S.run("pe", f, reads=bHG2[1] + bW_out[1], writes=[bPS[psi]])
                Xh = XBt[t_.xb][0:np_, s_, half * 512:(half + 1) * 512]
                S.run("dve", lambda e, Xh=Xh, psi=psi, np_=np_: e.tensor_tensor(out=Xh, in0=PS[psi][0:np_, :], in1=Xh, op=ALU.add),
                      reads=[bPS[psi], bXB[t_.xb][s_]], writes=[bXB[t_.xb][s_]])
                n -= 1

        def layer0(tl, next_tl):
            A1pe(0, tl, 0)
            A1evac(0, tl, 0)
            A1silu(0, tl, 0)
            CASTRI(tl, 0)
            TANHV(tl, 0)
            A1pe(0, tl, 1)
            for j in range(4):
                if j + 1 <= 3:
                    A1evac(0, tl, j + 1)
                drain1()
                if j == 3:
                    drain1(99)
                    out_part1(0, tl)
                E0a(tl, j)
                drain1()
                if j + 1 <= 3:
                    A1silu(0, tl, j + 1)
                    CASTRI(tl, j + 1)
                drain1()
                E0b(tl, j)
                drain1()
                if j + 1 <= 3:
                    TANHV(tl, j + 1)
                if j + 2 <= 3:
                    A1pe(0, tl, j + 2)

        def layer1(tl, next_tl):
            guard = []
            for bb in alias_bufs:
                guard.append(bb.w)
                guard.extend(bb.readers())
            po = PORD[1]
            A1pe(1, tl, po[0])
            A1evac(1, tl, po[0])
            A1silu(1, tl, po[0])
            S.wait_only("dve", guard)
            P1(tl, po[0])
            for i in range(4):
                j = po[i]
                jn = po[i + 1] if i + 1 <= 3 else None
                if i == 1:
                    flush_final(2)
                if i == 3 and next_tl is not None and next_tl.nsub == 4:
                    N1_batch(next_tl, "l0", subs=[2, 3])
                if jn is not None:
                    A1pe_xb(1, tl, jn)
                if i == 2 and next_tl is not None and not next_tl.prompt:
                    load_pool_rows()
                if i == 3 and not tl.defer1:
                    out_part1(1, tl)
                Z1pe(tl, j)
                if jn is not None:
                    A1pe_g(1, tl, jn)
                    A1evac(1, tl, jn)
                Z1act(tl, j)
                if jn is not None:
                    A1silu(1, tl, jn)
                    P1(tl, jn, only=0)
                Z1dve(tl, j)
                if jn is not None:
                    P1(tl, jn, only=1)
                if i == 3 and next_tl is not None and not next_tl.prompt:
                    pool_state_transposes("dve", (6, 7))
                flush_final({0: 2, 1: 0, 2: 0, 3: 0}[i])
                if i == 2 and next_tl is not None:
                    N1_batch(next_tl, "l0", subs=range(min(2, next_tl.nsub)))
                    for s in range(min(2, next_tl.nsub)):
                        N2(next_tl, s, ("l0", s))
            lastdve = Ev("E_dve", S.semcnt["E_dve"])
            for bb in alias_bufs:
                bb.add_read(lastdve)

        def bc(row_ap, w):
            return row_ap.unsqueeze(2).to_broadcast([128, NCH, w])

        def emit_rows(src_fn, bsrc, nrows, psa, psb, stage, bstage, dmas, sem, all_act=False, all_dve=False):
            for half, psi in ((0, psa), (1, psb)):
                def trf(e, half=half, psi=psi):
                    last = None
                    for cc in range(4):
                        c = half * 4 + cc
                        last = e.transpose(out=PS[psi][0:nrows, cc * 128:(cc + 1) * 128], in_=src_fn(c), identity=IDENT[:, :])
                    return last
                S.run("pe", trf, reads=[bsrc[half * 4 + cc] for cc in range(4)] + [bID], writes=[bPS[psi]])
                if all_dve or (half == 0 and not all_act):
                    S.run("dve", lambda e, psi=psi, half=half: e.tensor_copy(out=stage[0:nrows, half * 512:(half + 1) * 512],
                                                                             in_=PS[psi][0:nrows, :]),
                          reads=[bPS[psi]], writes=bstage)
                else:
                    S.run("act", lambda e, psi=psi, half=half: e.activation(
                        out=stage[0:nrows, half * 512:(half + 1) * 512], in_=PS[psi][0:nrows, :], func=AF.Copy),
                        reads=[bPS[psi]], writes=bstage)
            for (dst, r0, n) in dmas:
                S.dma(lambda e, dst=dst, r0=r0, n=n: e.dma_start(out=dst, in_=stage[r0:r0 + n, :]), sem, reads=bstage)

        SRO = [XBt[0][:, 1 + i, :] for i in range(3)]
        bSRO = [[bXB[0][1 + i]] for i in range(3)]

        def sample_layers(tl, which, prev_tl=None, hook=None, hook2=None):
            Tt, St = TS, NSEQ_S
            c8 = lambda ap, w: ap.rearrange("p (c t) -> p c t", t=w)
            allb = bSG + bV + bXC + bTR + bA2M + bH + bXCB + bEXT + bPTW
            guard = []
            for bb in allb:
                guard.append(bb.w)
                guard.extend(bb.readers())
            bs = {k: Buf("s_" + k) for k in ("ext", "sg", "v", "xc", "xcb", "tr", "m", "h", "pt0", "pt1", "zt", "pl", "t2")}

            def run(eng, fn, reads=(), writes=()):
                return S.run(eng, fn, reads=reads, writes=writes, deps=guard)

            xbps, gps = c8(PS[2][:, :], Tt), c8(PS[3][:, :], Tt)
            rps, ips = c8(PS[6][:, :], Tt), c8(PS[7][:, :], Tt)

            def inproj_all(l):
                for half, psi in ((0, 2), (1, 3)):
                    def f(e, half=half, psi=psi):
                        last = None
                        for c in range(NCH):
                            m = half * NCH + c
                            for k in range(NCH):
                                last = e.matmul(PS[psi][:, c * Tt:(c + 1) * Tt], lhsT=W_in[l][:, k, m * 128:(m + 1) * 128],
                                                rhs=UT[:, k, 0:Tt], start=(k == 0), stop=(k == NCH - 1))
                        return last
                    run("pe", f, reads=bUTs + bW_in[l], writes=[bPS[psi]])

            def act_warm(func):
                S.run("act", lambda e: e.activation(out=JUNK[:, 0:4], in_=NEGH[:, 0:4], func=func), reads=[bNEGH], writes=[bJUNK])

            if which == 0:
                HL = 3 * St
                EW = HL + Tt
                ext = c8(TMP[:, 0:NCH * EW], EW)
                xc = c8(TMP[:, 1024:1536], Tt)
                v = c8(TMP[:, 1536:2048], Tt)
                sg = c8(TMP[:, 2048:2560], Tt)
                t2 = c8(TMP[:, 2560:3072], Tt)
                tr, m, h = c8(TRt[0][:, :], Tt), c8(A2Mt[0][:, :], Tt), c8(Ht[0][:, :], Tt)
                xcb = c8(XCBt[0][:, :], Tt)
                inproj_all(0)
                run("act", lambda e: e.activation(out=ext[:, :, 0:HL], in_=H0s[:, :, :], func=AF.Copy), reads=bH0[False],
                    writes=[bs["ext"]])
                run("dve", lambda e: e.tensor_copy(out=ext[:, :, HL:EW], in_=xbps), reads=[bPS[2]], writes=[bs["ext"]])
                fpa = list(range(prev_tl.nsub)) if prev_tl is not None else []
                for _ in range(2):
                    if fpa:
                        final_piece_a(prev_tl, fpa.pop(0))
                run("act", lambda e: e.activation(out=sg, in_=gps, func=AF.Silu), reads=[bPS[3]], writes=[bs["sg"]])
                run("act", lambda e: e.activation(out=H0s[:, :, :], in_=ext[:, :, Tt:EW], func=AF.Copy), reads=[bs["ext"]],
                    writes=bH0[False])
                if fpa:
                    final_piece_a(prev_tl, fpa.pop(0))
                if hook is not None:
                    hook()
                run("dve", lambda e: e.tensor_tensor(out=xc, in0=ext[:, :, 0:Tt], in1=bc(VT[:, :, V_CW0], Tt), op=ALU.mult),
                    reads=[bs["ext"], bVT], writes=[bs["xc"]])
                for k in range(1, 4):
                    run("dve", lambda e, k=k: e.tensor_tensor(out=t2, in0=ext[:, :, k * St:k * St + Tt],
                                                              in1=bc(VT[:, :, V_CW0 + k], Tt), op=ALU.mult),
                        reads=[bs["ext"], bVT], writes=[bs["t2"]])
                    run("dve", lambda e: e.tensor_tensor(out=xc, in0=xc, in1=t2, op=ALU.add),
                        reads=[bs["xc"], bs["t2"]], writes=[bs["xc"]])
                run("dve", lambda e: e.tensor_tensor(out=xcb, in0=xc, in1=bc(VT[:, :, V_CB], Tt), op=ALU.add),
                    reads=[bs["xc"], bVT], writes=[bs["xcb"]])
                run("dve", lambda e: e.tensor_tensor(out=xc, in0=xc, in1=bc(VT[:, :, V_CB], Tt), op=ALU.add),
                    reads=[bs["xc"], bVT], writes=[bs["xc"]])
                while fpa:
                    final_piece_a(prev_tl, fpa.pop(0))
                fpb = list(range(prev_tl.nsub)) if prev_tl is not None else []

                def fri(e):
                    last = None
                    for c in range(NCH):
                        e.matmul(PS[6][:, c * Tt:(c + 1) * Tt], lhsT=W_r[:, c, :], rhs=xcb[:, c, :], start=True, stop=True)
                        last = e.matmul(PS[7][:, c * Tt:(c + 1) * Tt], lhsT=W_i[:, c, :], rhs=xcb[:, c, :], start=True, stop=True)
                    return last
                run("pe", fri, reads=[bs["xcb"], bW_r, bW_i], writes=[bPS[6], bPS[7]])
                run("dve", lambda e: e.scalar_tensor_tensor(out=tr, in0=rps, scalar=0.5, in1=bc(HBR, Tt), op0=ALU.mult,
                                                            op1=ALU.add), reads=[bPS[6], bDER], writes=[bs["tr"]])
                run("dve", lambda e: e.scalar_tensor_tensor(out=v, in0=ips, scalar=0.5, in1=bc(HBI, Tt), op0=ALU.mult,
                                                            op1=ALU.add), reads=[bPS[7], bDER], writes=[bs["v"]])
                run("act", lambda e: e.activation(out=tr, in_=tr, func=AF.Tanh), reads=[bs["tr"]], writes=[bs["tr"]])
                run("act", lambda e: e.activation(out=v, in_=v, func=AF.Tanh), reads=[bs["v"]], writes=[bs["v"]])
                act_warm(AF.Exp)
                run("dve", lambda e: e.scalar_tensor_tensor(out=v, in0=v, scalar=1.0, in1=xc, op0=ALU.add, op1=ALU.mult),
                    reads=[bs["v"], bs["xc"]], writes=[bs["v"]])
                run("dve", lambda e: e.scalar_tensor_tensor(out=tr, in0=tr, scalar=1.0, in1=bc(K1, Tt), op0=ALU.add,
                                                            op1=ALU.mult), reads=[bs["tr"], bDER], writes=[bs["tr"]])
                run("act", lambda e: e.activation(out=m, in_=tr, func=AF.Exp, scale=2.0), reads=[bs["tr"]], writes=[bs["m"]])
                run("act", lambda e: e.activation(out=m, in_=m, func=AF.Ln, scale=-1.0, bias=1.0), reads=[bs["m"]], writes=[bs["m"]])
                run("act", lambda e: e.activation(out=m, in_=m, func=AF.Exp, scale=0.5, bias=LN_HALF), reads=[bs["m"]],
                    writes=[bs["m"]])
                run("act", lambda e: e.activation(out=tr, in_=tr, func=AF.Exp), reads=[bs["tr"]], writes=[bs["tr"]])
                for _ in range(2):
                    if fpb:
                        final_piece_b(prev_tl, fpb.pop(0), None)
                run("dve", lambda e: e.tensor_tensor(out=v, in0=v, in1=m, op=ALU.mult), reads=[bs["v"], bs["m"]], writes=[bs["v"]])
                for st in range(STEPS):
                    sl = slice(st * St, (st + 1) * St)
                    prev = HSTs[:, :, :] if st == 0 else h[:, :, (st - 1) * St:st * St]
                    run("dve", lambda e, sl=sl, prev=prev: e.tensor_tensor(out=h[:, :, sl], in0=tr[:, :, sl], in1=prev, op=ALU.mult),
                        reads=[bs["tr"], bs["h"]] + bHST[False], writes=[bs["h"]])
                    run("dve", lambda e, sl=sl: e.tensor_tensor(out=h[:, :, sl], in0=h[:, :, sl], in1=v[:, :, sl], op=ALU.add),
                        reads=[bs["v"], bs["h"]], writes=[bs["h"]])
                run("act", lambda e: e.activation(out=HSTs[:, :, :], in_=h[:, :, Tt - St:Tt], func=AF.Copy), reads=[bs["h"]],
                    writes=bHST[False])
                run("dve", lambda e: e.tensor_tensor(out=HGt[1][:, :, 0:Tt], in0=h, in1=sg, op=ALU.mult),
                    reads=[bs["h"], bs["sg"]], writes=bHG2[1])
                while fpb:
                    final_piece_b(prev_tl, fpb.pop(0), None)
            else:
                HL = 15 * St
                EW = HL + Tt
                ext = c8(TMP[:, 0:NCH * EW], EW)
                pt = [c8(TMP[:, 2432:2432 + 2 * EW], EW), c8(TMP[:, 2432 + 2 * EW:2432 + 4 * EW], EW)]
                sg, zt = c8(Ht[0][:, :], Tt), c8(Ht[1][:, :], Tt)
                pl = c8(XCBt[0][:, :], Tt)
                inproj_all(1)
                act_warm(AF.Silu)
                run("act", lambda e: e.activation(out=ext[:, :, 0:HL], in_=H1s[:, :, :], func=AF.Copy), reads=bH1[False],
                    writes=[bs["ext"]])
                run("dve", lambda e: e.tensor_copy(out=ext[:, :, HL:EW], in_=xbps), reads=[bPS[2]], writes=[bs["ext"]])
                run("act", lambda e: e.activation(out=sg, in_=gps, func=AF.Silu), reads=[bPS[3]], writes=[bs["sg"]])
                run("act", lambda e: e.activation(out=H1s[:, :, :], in_=ext[:, :, Tt:EW], func=AF.Copy), reads=[bs["ext"]],
                    writes=bH1[False])
                if hook is not None:
                    hook()
                for g, Wd in enumerate(POOL_W):
                    cs = slice(2 * g, 2 * g + 2)
                    src = ext[:, cs, :]
                    bsrc = bs["ext"]
                    lvl, di = 1, 0
                    while lvl < Wd:
                        lo = HL - (Wd - 2 * lvl) * St
                        sh = lvl * St
                        dst = pt[di]
                        run("dve", lambda e, dst=dst, src=src, lo=lo, sh=sh: e.tensor_tensor(
                            out=dst[:, :, lo:EW], in0=src[:, :, lo:EW], in1=src[:, :, lo - sh:EW - sh], op=ALU.add),
                            reads=[bsrc], writes=[bs["pt%d" % di]])
                        src, bsrc = dst, bs["pt%d" % di]
                        di ^= 1
                        lvl *= 2
                    run("dve", lambda e, src=src, cs=cs, Wd=Wd: e.scalar_tensor_tensor(
                        out=pl[:, cs, :], in0=src[:, :, HL:EW], scalar=1.0 / Wd, in1=ext[:, cs, HL:EW],
                        op0=ALU.mult, op1=ALU.subtract), reads=[bsrc, bs["ext"]], writes=[bs["pl"]])

                def fz(e):
                    last = None
                    for g in range(4):
                        for mi in range(2):
                            c = 2 * g + mi
                            for kc in range(2):
                                last = e.matmul(PS[6][:, c * Tt:(c + 1) * Tt], lhsT=W_grp[:, 2 * g + kc, mi * 128:(mi + 1) * 128],
                                                rhs=pl[:, 2 * g + kc, :], start=(kc == 0), stop=(kc == 1))
                    return last
                run("pe", fz, reads=[bs["pl"], bW_grp], writes=[bPS[6]])
                if hook2 is not None:
                    hook2()
                run("dve", lambda e: e.tensor_tensor(out=zt, in0=rps, in1=bc(VT[:, :, V_BSCALE], Tt), op=ALU.mult),
                    reads=[bPS[6], bVT], writes=[bs["zt"]])
                run("dve", lambda e: e.tensor_tensor(out=zt, in0=zt, in1=bc(BGS, Tt), op=ALU.add),
                    reads=[bs["zt"], bDER], writes=[bs["zt"]])
                run("dve", lambda e: e.tensor_tensor(out=HGt[1][:, :, 0:Tt], in0=zt, in1=sg, op=ALU.mult),
                    reads=[bs["zt"], bs["sg"]], writes=bHG2[1])
            for bb in bs.values():
                evs = [bb.w] + bb.readers()
                for tgt in allb:
                    for ev in evs:
                        tgt.add_read(ev)

        SRX = [XBt[1][:, i, :] for i in range(4)]
        for j in range(3):
            S.dma(lambda e, j=j: e.dma_start(out=SRX[0][j * 16:(j + 1) * 16, :], in_=st_conv[:, j, :]), "ld_sr",
                  writes=bXB[1])
        S.dma(lambda e: e.dma_start(out=SRX[1][0:16, :], in_=st_lru), "ld_sr", writes=bXB[1])
        S.newsem("ld_pool")

        def load_pool_rows():
            gate = []
            for bb in (bXB[0][2], bXB[0][3]):
                gate.append(bb.w)
                gate.extend(bb.readers())
            ev = None
            for j in range(15):
                dst = XBt[0][:, 2, :] if j < 8 else XBt[0][:, 3, :]
                jj = j if j < 8 else j - 8
                ev = S.dma(lambda e, j=j, jj=jj, dst=dst: e.dma_start(out=dst[jj * 16:(jj + 1) * 16, :], in_=st_pool[:, j, :]),
                           "ld_pool", deps=gate)
            for bb in (bXB[0][2], bXB[0][3]):
                bb.w = ev
                bb.r = {}

        def rows_to_fm4(src, bsrc, nrows, dst4_fn, bdst_all, psis, eng):
            for half in range(2):
                psi = psis[half]

                def trf(e, half=half, psi=psi):
                    last = None
                    for cc in range(4):
                        c = half * 4 + cc
                        last = e.transpose(out=PS[psi][:, cc * 128:cc * 128 + nrows],
                                           in_=src[0:nrows, c * 128:(c + 1) * 128],
                                           identity=IDENT[0:nrows, 0:nrows])
                    return last
                S.run("pe", trf, reads=list(bsrc) + [bID], writes=[bPS[psi]])
                pv = PS[psi][:, :].rearrange("p (c r) -> p c r", r=128)[:, :, 0:nrows]
                if eng == "dve":
                    S.run("dve", lambda e, half=half, pv=pv: e.tensor_copy(out=dst4_fn(half), in_=pv),
                          reads=[bPS[psi]], writes=bdst_all)
                else:
                    S.run("act", lambda e, half=half, pv=pv: e.activation(out=dst4_fn(half), in_=pv, func=AF.Copy),
                          reads=[bPS[psi]], writes=bdst_all)

        def pool_state_transposes(eng="dve", psis=(0, 1)):
            bH1all = bH1[False] + bHG2[0]
            rows_to_fm4(XBt[0][:, 2, :], [bXB[0][2], bXB[0][3]], 128, lambda h: H1s[:, 4 * h:4 * h + 4, 0:128], bH1all, psis, eng)
            rows_to_fm4(XBt[0][:, 3, :], [bXB[0][2], bXB[0][3]], 112, lambda h: H1s[:, 4 * h:4 * h + 4, 128:240], bH1all, psis, eng)

        def state_transposes():
            rows_to_fm(SRX[0], bXB[1], 48, lambda c: H0s[:, c, :], bH0[False], 6)
            rows_to_fm(SRX[1], bXB[1], 16, lambda c: HSTs[:, c, :], bHST[False], 7)

        n2_override[0] = "dve"
        norm_pipeline(tiles[0], "l0")
        n2_override[0] = None
        late_prologue()
        prompt_states_pending = [False]
        for ti, tl in enumerate(tiles):
            nxt = tiles[ti + 1] if ti + 1 < len(tiles) else None
            tl.opened = tl.prompt
            tl.defer1 = tl.prompt and nxt is not None and nxt.prompt
            if tl.prompt:
                layer0(tl, nxt)
            else:
                prev_t = pending_final[0][0] if pending_final else None
                del pending_final[:]
                sample_layers(tl, 0, prev_t)
            while wq:
                maybe_casts()
            for s in range(tl.nsub):
                out_part2(0, tl, s)
            if not tl.prompt:
                S.newsem("st_m7")
                emit_rows(lambda c: H0s[:, c, :], bH0[False], 48, 4, 5, SRO[0], bSRO[0],
                          [(o_conv_s[:, j, :], j * 16, 16) for j in range(3)], "st_m3", all_dve=True)
                emit_rows(lambda c: HSTs[:, c, :], bHST[False], 16, 4, 5, SRO[1], bSRO[1], [(o_lru_s, 0, 16)], "st_m7", all_dve=True)
            norm_pipeline(tl, "l1")
            if ti == 0:
                state_transposes()
                load_x(tiles[1])
            if tl.prompt:
                layer1(tl, nxt)
            else:
                for k_ in ("st_m4", "st_m5", "st_m6"):
                    S.newsem(k_)

                def emit_prompt_states():
                    emit_rows(lambda c: H0p[:, c, :], bH0[True], 3, 4, 5, SRO[1], bSRO[1], [(o_conv_p, 0, 3)], "st_m4", all_act=True)
                    emit_rows(lambda c: HSTp[:, c, :], bHST[True], 1, 4, 5, SRO[2], bSRO[2], [(o_lru_p, 0, 1)], "st_m5", all_act=True)

                    emit_rows(lambda c: H1p[:, c, :], bH1[True], 15, 4, 5, SRO[0], bSRO[0], [(o_pool_p, 0, 15)], "st_m6", all_act=True)

                def emit_states2():
                    emit_rows(lambda c: H1s[:, c, 0:128], bH1[False], 128, 4, 5, SRO[1], bSRO[1],
                              [(o_pool_s[:, j, :], j * 16, 16) for j in range(8)], "st_m0", all_act=True)
                    emit_rows(lambda c: H1s[:, c, 128:240], bH1[False], 112, 4, 5, SRO[2], bSRO[2],
                              [(o_pool_s[:, j, :], (j - 8) * 16, 16) for j in range(8, 15)], "st_m2", all_act=True)
                sample_layers(tl, 1, hook=emit_prompt_states, hook2=emit_states2)
            if tl.defer1:
                for s in range(tl.nsub):
                    for half in range(2):
                        drainq.append((tl, s, half))
            else:
                for s in range(tl.nsub):
                    out_part2(1, tl, s)
            if False:
                pass
            if tl.prompt and nxt is not None and not nxt.prompt:
                prompt_states_pending[0] = True
            if nxt is not None:
                ns = nxt.nsub
                norm_pipeline(nxt, "l0", n1_done=range(ns), n2_done=range(min(2, ns)))
            nn = tiles[ti + 2] if ti + 2 < len(tiles) else None
            for s_ in range(tl.nsub):
                pending_final.append((tl, s_, nn))
            if nxt is None:
                flush_final()

        fin = [Ev(k, S.semcnt[k]) for k in S.semcnt if (k.startswith("stx") or k.startswith("st_m")) and S.semcnt[k] > 0]
        S.wait_only("sp", fin)
        S.emit()
    return nc


_VEC_ROWS = None


def kernel(**inputs):
    f32 = np.float32
    g = {k: np.asarray(v) for k, v in inputs.items()}
    vecs = np.zeros((16, D), f32)
    vecs[V_ANORM] = g["a_norm"][0]
    vecs[V_CW0:V_CW0 + 4] = g["a_conv_w"][0]
    vecs[V_CB] = g["a_conv_b"][0]
    vecs[V_BR] = g["a_b_r"][0]
    vecs[V_BI] = g["a_b_i"][0]
    vecs[V_LAM] = g["a_lam"][0]
    vecs[V_BNORM] = g["b_norm"][0]
    vecs[V_BGRP] = g["b_b_grp"][0]
    vecs[V_BSCALE] = g["b_scale"][0]
    shared = {
        "vecs": vecs,
        "fnorm": np.ascontiguousarray(g["final_norm"].reshape(1, D).astype(f32)),
        "a_w_in": np.ascontiguousarray(g["a_w_in"][0], f32),
        "a_w_r": np.ascontiguousarray(g["a_w_r"][0], f32),
        "a_w_i": np.ascontiguousarray(g["a_w_i"][0], f32),
        "a_w_out": np.ascontiguousarray(g["a_w_out"][0], f32),
        "b_w_in": np.ascontiguousarray(g["b_w_in"][0], f32),
        "b_w_grp": np.ascontiguousarray(g["b_w_grp"][0], f32),
        "b_w_out": np.ascontiguousarray(g["b_w_out"][0], f32),
    }
    in_maps = []
    for c in range(NCORES):
        sl = slice(c * NSEQ_S, (c + 1) * NSEQ_S)
        m = dict(shared)
        m["x_p"] = np.ascontiguousarray(g["x_prompt"][c], f32)
        m["x_s"] = np.ascontiguousarray(g["x_sample"][sl], f32)
        m["st_conv"] = np.ascontiguousarray(g["state_conv"][0, sl], f32)
        m["st_lru"] = np.ascontiguousarray(g["state_lru"][0, sl], f32)
        m["st_pool"] = np.ascontiguousarray(g["state_pool"][0, sl], f32)
        in_maps.append(m)
    nc = build_program()
    res = run_bass_kernel_spmd(nc, in_maps, core_ids=list(range(NCORES)))
    R = res.results
    y_prompt = np.stack([R[c]["y_p"] for c in range(NCORES)], 0)
    y_sample = np.concatenate([R[c]["y_s"] for c in range(NCORES)], 0)
    conv_p = np.stack([R[c]["o_conv_p"] for c in range(NCORES)], 0)[None]
    lru_p = np.concatenate([R[c]["o_lru_p"] for c in range(NCORES)], 0)[None]
    pool_p = np.stack([R[c]["o_pool_p"] for c in range(NCORES)], 0)[None]
    conv_s = np.concatenate([R[c]["o_conv_s"] for c in range(NCORES)], 0)[None]
    lru_s = np.concatenate([R[c]["o_lru_s"] for c in range(NCORES)], 0)[None]
    pool_s = np.concatenate([R[c]["o_pool_s"] for c in range(NCORES)], 0)[None]
    return tuple(np.ascontiguousarray(a.astype(f32)) for a in
                 (y_prompt, y_sample, conv_p, lru_p, pool_p, conv_s, lru_s, pool_s))
```
